# Optimizing a Trainium2 kernel written in Bass

```python
import jax, jax.numpy as jnp
from jax import lax
import numpy as np

D_MODEL = 1024
BATCH = 4
SEQ = 4096
DEPTH = 2
DEC_BATCH = 32
DEC_SEQ = 2048
PAST_LEN = 128

N_MEM = 256
XA_HEADS = 4
XA_HEAD_DIM = D_MODEL // XA_HEADS
CONV_WIDTH = 31
D_CONV = D_MODEL // 2
D_FOURIER = D_MODEL - D_CONV
FOURIER_GROUPS = 4
FOURIER_GROUP_DIM = D_FOURIER // FOURIER_GROUPS
POOL_WINDOWS = (2, 4, 8, 16)
POOL_GROUPS = len(POOL_WINDOWS)
POOL_GROUP_DIM = D_MODEL // POOL_GROUPS
D_FF = ((8 * D_MODEL + 3 * 256 - 1) // (3 * 256)) * 256
N_EVEN = (DEPTH + 1) // 2
N_ODD = DEPTH // 2
RMS_EPS = 1e-6
LN_EPS = 1e-5

kernel_name = 'hybrid_conv_fourier_pool_encoder'


def rms_norm(x, g):
    xf = x.astype(jnp.float32)
    y = xf * lax.rsqrt(jnp.mean(xf * xf, axis=-1, keepdims=True) + RMS_EPS)
    return (y * g.astype(jnp.float32)).astype(x.dtype)


def layer_norm(x, g, b):
    xf = x.astype(jnp.float32)
    mu = jnp.mean(xf, axis=-1, keepdims=True)
    xc = xf - mu
    y = xc * lax.rsqrt(jnp.mean(xc * xc, axis=-1, keepdims=True) + LN_EPS)
    return (y * g.astype(jnp.float32) + b.astype(jnp.float32)).astype(x.dtype)


def conformer_conv(u_val, u_gate, w_dw, b_dw, ln_g, ln_b):
    h = u_val * jax.nn.sigmoid(u_gate)
    h = lax.conv_general_dilated(
        h, w_dw[:, None, :], window_strides=(1,),
        padding=[(CONV_WIDTH // 2, CONV_WIDTH // 2)],
        dimension_numbers=('NWC', 'WIO', 'NWC'),
        feature_group_count=D_CONV) + b_dw
    return jax.nn.silu(layer_norm(h, ln_g, ln_b))


def fourier_mix(u):
    b_, s_, _ = u.shape
    ug = u.reshape(b_, s_, FOURIER_GROUPS, FOURIER_GROUP_DIM).astype(jnp.float32)
    f = jnp.fft.fftn(ug, axes=(1, 3), norm='ortho').real
    return f.reshape(b_, s_, D_FOURIER).astype(u.dtype)


def multiscale_pool(h, w_grp, scale):
    b_, s_, _ = h.shape
    hf = h.astype(jnp.float32)
    csum = jnp.concatenate([jnp.zeros_like(hf[:, :1]), jnp.cumsum(hf, axis=1)], axis=1)
    pos = jnp.arange(s_)
    outs = []
    for g, w in enumerate(POOL_WINDOWS):
        c0 = g * POOL_GROUP_DIM
        c = csum[:, :, c0:c0 + POOL_GROUP_DIM]
        lo = jnp.clip(pos - w // 2, 0, s_)
        hi = jnp.clip(pos + w - w // 2, 0, s_)
        cnt = (hi - lo).astype(jnp.float32)[None, :, None]
        outs.append((c[:, hi] - c[:, lo]) / cnt - hf[:, :, c0:c0 + POOL_GROUP_DIM])
    p = jnp.stack(outs, axis=2).astype(h.dtype)
    y = jnp.einsum('bsgc,gcd->bsgd', p, w_grp).reshape(b_, s_, D_MODEL)
    return y * scale


def memory_cross_attn(h, mem_n, wq, wkv, wo):
    b_, s_, _ = h.shape
    m_ = mem_n.shape[1]
    q = (h @ wq).reshape(b_, s_, XA_HEADS, XA_HEAD_DIM)
    kv = (mem_n @ wkv).reshape(b_, m_, 2, XA_HEADS, XA_HEAD_DIM)
    k, v = kv[:, :, 0], kv[:, :, 1]
    scores = jnp.einsum('bshd,bmhd->bhsm', q, k).astype(jnp.float32) * (XA_HEAD_DIM ** -0.5)
    probs = jax.nn.softmax(scores, axis=-1).astype(v.dtype)
    o = jnp.einsum('bhsm,bmhd->bshd', probs, v).reshape(b_, s_, D_MODEL)
    return o @ wo


def swiglu(h, w_gate_up, w_down):
    gu = h @ w_gate_up
    g, u = gu[..., :D_FF], gu[..., D_FF:]
    return (jax.nn.silu(g) * u) @ w_down


def trunk(x, mem, norm_mix, w_in_even, conv_w, conv_b, conv_ln_g, conv_ln_b, w_out_even,
          w_pool, pool_scale, norm_xa, norm_mem, xa_wq, xa_wkv, xa_wo, norm_ffn,
          ffn_w_gate_up, ffn_w_down, norm_final):
    for l in range(DEPTH):
        h = rms_norm(x, norm_mix[l])
        if l % 2 == 0:
            e = l // 2
            u = h @ w_in_even[e]
            a = conformer_conv(u[..., :D_CONV], u[..., D_CONV:2 * D_CONV],
                               conv_w[e], conv_b[e], conv_ln_g[e], conv_ln_b[e])
            f = fourier_mix(u[..., 2 * D_CONV:])
            x = x + jnp.concatenate([a, f], axis=-1) @ w_out_even[e]
        else:
            o = l // 2
            x = x + multiscale_pool(h, w_pool[o], pool_scale[o])
        h = rms_norm(x, norm_xa[l])
        m = rms_norm(mem, norm_mem[l])
        x = x + memory_cross_attn(h, m, xa_wq[l], xa_wkv[l], xa_wo[l])
        h = rms_norm(x, norm_ffn[l])
        x = x + swiglu(h, ffn_w_gate_up[l], ffn_w_down[l])
    return rms_norm(x, norm_final)


def setup_inputs(seed: int = 0) -> dict:
    key = jax.random.key(seed)
    ks = jax.random.split(key, 24)
    f32 = jnp.float32
    nrm = lambda k, shape, s: jax.random.normal(k, shape, f32) * s
    gain = lambda k, shape: 1.0 + 0.05 * jax.random.normal(k, shape, f32)
    d_in_even = 2 * D_CONV + D_FOURIER
    return {
        'x_prompt': nrm(ks[0], (BATCH, SEQ, D_MODEL), 1.0),
        'x_sample': nrm(ks[1], (DEC_BATCH, DEC_SEQ, D_MODEL), 1.0),
        'mem_prompt': nrm(ks[2], (BATCH, N_MEM, D_MODEL), 1.0),
        'mem_sample': nrm(ks[3], (DEC_BATCH, N_MEM, D_MODEL), 1.0),
        'norm_mix': gain(ks[4], (DEPTH, D_MODEL)),
        'w_in_even': nrm(ks[5], (N_EVEN, D_MODEL, d_in_even), D_MODEL ** -0.5),
        'conv_w': nrm(ks[6], (N_EVEN, CONV_WIDTH, D_CONV), CONV_WIDTH ** -0.5),
        'conv_b': nrm(ks[7], (N_EVEN, D_CONV), 0.01),
        'conv_ln_g': gain(ks[8], (N_EVEN, D_CONV)),
        'conv_ln_b': nrm(ks[9], (N_EVEN, D_CONV), 0.01),
        'w_out_even': nrm(ks[10], (N_EVEN, D_CONV + D_FOURIER, D_MODEL), (D_CONV + D_FOURIER) ** -0.5),
        'w_pool': nrm(ks[11], (N_ODD, POOL_GROUPS, POOL_GROUP_DIM, POOL_GROUP_DIM), POOL_GROUP_DIM ** -0.5),
        'pool_scale': 0.5 + 0.1 * jax.random.normal(ks[12], (N_ODD, D_MODEL), f32),
        'norm_xa': gain(ks[13], (DEPTH, D_MODEL)),
        'norm_mem': gain(ks[14], (DEPTH, D_MODEL)),
        'xa_wq': nrm(ks[15], (DEPTH, D_MODEL, D_MODEL), D_MODEL ** -0.5),
        'xa_wkv': nrm(ks[16], (DEPTH, D_MODEL, 2 * D_MODEL), D_MODEL ** -0.5),
        'xa_wo': nrm(ks[17], (DEPTH, D_MODEL, D_MODEL), D_MODEL ** -0.5),
        'norm_ffn': gain(ks[18], (DEPTH, D_MODEL)),
        'ffn_w_gate_up': nrm(ks[19], (DEPTH, D_MODEL, 2 * D_FF), D_MODEL ** -0.5),
        'ffn_w_down': nrm(ks[20], (DEPTH, D_FF, D_MODEL), D_FF ** -0.5),
        'norm_final': gain(ks[21], (D_MODEL,)),
    }


def reference(x_prompt, x_sample, mem_prompt, mem_sample, norm_mix, w_in_even, conv_w, conv_b,
              conv_ln_g, conv_ln_b, w_out_even, w_pool, pool_scale, norm_xa, norm_mem, xa_wq,
              xa_wkv, xa_wo, norm_ffn, ffn_w_gate_up, ffn_w_down, norm_final):
    y_prompt = trunk(x_prompt, mem_prompt, norm_mix, w_in_even, conv_w, conv_b, conv_ln_g,
                     conv_ln_b, w_out_even, w_pool, pool_scale, norm_xa, norm_mem, xa_wq,
                     xa_wkv, xa_wo, norm_ffn, ffn_w_gate_up, ffn_w_down, norm_final)
    y_sample = trunk(x_sample, mem_sample, norm_mix, w_in_even, conv_w, conv_b, conv_ln_g,
                     conv_ln_b, w_out_even, w_pool, pool_scale, norm_xa, norm_mem, xa_wq,
                     xa_wkv, xa_wo, norm_ffn, ffn_w_gate_up, ffn_w_down, norm_final)
    return (y_prompt, y_sample)
```

```python
import contextlib
import numpy as np
import ml_dtypes
import concourse.bass as bass
import concourse.mybir as mybir
from concourse.bass_utils import run_bass_kernel_spmd

F32 = mybir.dt.float32
BF16 = mybir.dt.bfloat16
AF = mybir.ActivationFunctionType
ALU = mybir.AluOpType

NCORES = 8
D = 1024
KC = 8
UT = 1024
HAL = 16
XC = UT + 2 * HAL
NUNIT = 10
DFF = 2816
FC = 22
NMEM = 256
CW = 31
RMS_EPS = 1e-6
LN_EPS = 1e-5
TILES = ((0, 512), (512, 512), (1024, 32))

ENGS = ("pe", "act", "dve", "pool", "sp")
N_DMA_SEMS = 8
SAME_ENGINE_SYNC = True

V_NMIX, V_NXA, V_NMEM, V_NFFN = 0, 16, 32, 48
V_NFIN, V_PSC = 64, 72
V_CB, V_LG, V_LB = 80, 84, 88
V_CW = 92
NVEC = 92 + 4 * CW
XA_SCALE = 256 ** -0.5


class Op:
    __slots__ = ("eng", "fn", "reads", "writes", "dma", "idx", "sig", "count", "deps", "dsem", "dcount",
                 "prev_on_sem", "stage")

    def __init__(self, eng, fn, reads, writes, dma):
        self.eng, self.fn, self.reads, self.writes, self.dma = eng, fn, reads, writes, dma
        self.sig = False
        self.count = None
        self.deps = []
        self.dsem = None
        self.dcount = None
        self.prev_on_sem = None


class Prog:
    def __init__(self):
        self.ops = {e: [] for e in ENGS}
        self.last_w = {}
        self.readers = {}
        self.n = 0
        self.stage = "init"

    @staticmethod
    def _atoms(rs):
        out = []
        for r in rs:
            if isinstance(r, tuple) and len(r) == 3 and r[0] == "A":
                out.extend(("A", a) for a in range(r[1] // 256, (r[2] + 255) // 256))
            else:
                out.append(r)
        return out

    def op(self, eng, fn, reads=(), writes=(), dma=False):
        o = Op(eng, fn, self._atoms(reads), self._atoms(writes), dma)
        o.idx = self.n
        o.stage = self.stage
        self.n += 1
        deps = set()
        for r in o.reads:
            w = self.last_w.get(r)
            if w is not None:
                deps.add(w)
        for r in o.writes:
            w = self.last_w.get(r)
            if w is not None:
                deps.add(w)
            rd = self.readers.get(r)
            if rd:
                deps.update(rd.values())
        for r in o.reads:
            rd = self.readers.setdefault(r, {})
            rd[("dma", o.idx) if dma else eng] = o
        for r in o.writes:
            self.last_w[r] = o
            self.readers[r] = {}
        deps.discard(o)
        keep = []
        for d in deps:
            if d.eng == eng and not d.dma and not dma:
                if eng == "pe" or not SAME_ENGINE_SYNC:
                    continue
            keep.append(d)
        o.deps = keep
        for d in keep:
            d.sig = True
        self.ops[eng].append(o)
        return o

    def emit(self, nc, final_wait_ops=()):
        with contextlib.ExitStack() as st:
            esem = {e: st.enter_context(nc.semaphore("s_" + e)) for e in ENGS}
            dsems = {e: [st.enter_context(nc.semaphore("d_%s%d" % (e, i))) for i in range(N_DMA_SEMS)]
                     for e in ("sp", "act", "pool")}
            for e in ENGS:
                c = 0
                k = 0
                lastd = [None] * N_DMA_SEMS
                dcnt = [0] * N_DMA_SEMS
                for o in self.ops[e]:
                    if o.dma:
                        s = k % N_DMA_SEMS
                        k += 1
                        o.dsem = dsems[e][s]
                        dcnt[s] += 16
                        o.dcount = dcnt[s]
                        o.prev_on_sem = lastd[s]
                        lastd[s] = o
                    elif o.sig:
                        c += 1
                        o.count = c
            block = st.enter_context(nc.Block())

            def run_stream(e):
                def body(eng):
                    known = {}

                    def wait_for(d):
                        if d.dma:
                            key, sem, val = ("d", id(d.dsem)), d.dsem, d.dcount
                        else:
                            key, sem, val = ("e", d.eng), esem[d.eng], d.count
                        if known.get(key, 0) >= val:
                            return
                        eng.wait_ge(sem, val)
                        known[key] = val

                    for o in self.ops[e]:
                        for d in sorted(o.deps, key=lambda d: d.idx):
                            wait_for(d)
                        if o.dma and o.prev_on_sem is not None:
                            wait_for(o.prev_on_sem)
                        ins = o.fn(eng)
                        if o.dma:
                            ins.then_inc(o.dsem, 16)
                        elif o.sig:
                            ins.then_inc(esem[e], 1)
                    if e == "sp":
                        for d in final_wait_ops:
                            wait_for(d)
                return body

            block.tensor(run_stream("pe"))
            block.scalar(run_stream("act"))
            block.vector(run_stream("dve"))
            block.gpsimd(run_stream("pool"))
            block.sync(run_stream("sp"))


def I(m, *a, **k):
    return lambda e: getattr(e, m)(*a, **k)


class ABuf:
    def __init__(self, arena, off, n0, n1, dt):
        self.esz = 2 if dt == F32 else 1
        self.off, self.n0, self.n1 = off, n0, n1
        sl = arena[:, off:off + n0 * n1 * self.esz]
        if dt == F32:
            sl = sl.bitcast(F32)
        self.ap = sl.rearrange("p (a b) -> p a b", a=n0)
        self.end = off + n0 * n1 * self.esz

    def res(self, i, c0=None, c1=None):
        b = self.off + i * self.n1 * self.esz
        if c0 is None:
            return ("A", b, b + self.n1 * self.esz)
        return ("A", b + c0 * self.esz, b + c1 * self.esz)

    def allres(self):
        return [("A", self.off, self.end)]


_PROG = {}


def build_program():
    nc = bass.Bass("TRN2", target_bir_lowering=False)

    def din(name, shape, dt=F32):
        return nc.dram_tensor(name, list(shape), dt, kind="ExternalInput").ap()

    xu = din("xu", [NUNIT, UT, D])
    xop = din("xop", [2048, D])
    xh = din("xh", [NUNIT, 64, D])
    mem = din("mem", [5, NMEM, D])
    tabP = din("tabP", [2, 2, 2048, XC], BF16)
    tabS = din("tabS", [2, 2, 1024, XC], BF16)
    corr_d = din("corr", [NUNIT, 128, 128])
    vmask_d = din("vmask", [NUNIT, 128, 32])
    vecs_d = din("vecs", [128, NVEC])
    ident_d = din("ident", [128, 128])
    identb_d = din("identb", [128, 128], BF16)
    c128_d = din("c128", [128, 128], BF16)
    s128n_d = din("s128n", [128, 128], BF16)
    w_in = din("w_in_even", [1, D, 1536])
    w_out = din("w_out_even", [1, D, D])
    w_pool = din("w_pool", [1, 4, 256, 256])
    wq = din("xa_wq", [2, D, D])
    wkv = din("xa_wkv", [2, D, 2 * D])
    wo = din("xa_wo", [2, D, D])
    wgu = din("ffn_w_gate_up", [2, D, 2 * DFF])
    wdn = din("ffn_w_down", [2, DFF, D])
    yu = nc.dram_tensor("yu", [NUNIT, UT, D], F32, kind="ExternalOutput").ap()

    def dscratch(name, shape):
        return nc.dram_tensor(name, list(shape), BF16, kind="Internal").ap()

    b_in = dscratch("b_in", [D, 1536])
    b_out = dscratch("b_out", [D, D])
    b_pool = dscratch("b_pool", [1024, 256])
    b_q = dscratch("b_q", [2, D, D])
    b_kv = dscratch("b_kv", [2, D, 2 * D])
    b_o = dscratch("b_o", [2, D, D])
    b_gu = dscratch("b_gu", [2, D, 2 * DFF])
    b_dn = dscratch("b_dn", [2, DFF, D])
    b_diag = dscratch("b_diag", [4, CW * 128, 128])
    u_scr = dscratch("u_scr", [5, 32 * 128, 512])
    kv_scr = dscratch("kv_scr", [5, 2, 128, 4096])

    st = contextlib.ExitStack()
    with st:
        def T(name, shape, dt):
            return st.enter_context(nc.sbuf_tensor(name, list(shape), dt))

        xT = T("xT", [128, KC, XC], F32)
        xhT = T("xhT", [128, KC, 64], F32)
        hbuf = [T("hT0", [128, KC, 512], BF16), T("hT1", [128, KC, 512], BF16), T("hT2", [128, KC, 64], BF16)]
        RING_N = 4
        ring = [T("ring%d" % i, [128, 4096], BF16) for i in range(RING_N)]
        stg = [T("stg%d" % i, [128, D], F32) for i in range(3)]
        ostg = [T("ostg%d" % i, [128, D], F32) for i in range(2)]
        sq = T("sq", [128, KC, 512], BF16)
        rstd = [T("rstd%d" % i, [128, 512], F32) for i in range(2)]
        ident = T("identS", [128, 128], F32)
        identb = T("identbS", [128, 128], BF16)
        ones = T("onesS", [128, 128], BF16)
        ones512 = T("ones512S", [128, 128], BF16)
        ones1 = T("ones1S", [128, 128], BF16)
        c128 = T("c128S", [128, 128], BF16)
        s128n = T("s128nS", [128, 128], BF16)
        vecs = T("vecsS", [128, NVEC], F32)
        corr = T("corrS", [128, KC, 16], F32)
        vmask = T("vmaskS", [128, 32], F32)
        AREN = 43776
        arena = T("arena", [128, AREN], BF16)
        ps = [st.enter_context(nc.psum_tensor("ps%d" % i, [128, 512], F32)) for i in range(8)]

        P = Prog()
        state = {"bank": 0, "ring": 0, "stg": 0, "ostg": 0, "par": 0, "rs": 0, "ev": 0, "cst": 0}

        def bank():
            b = state["bank"]
            state["bank"] = (b + 1) % 8
            return b

        def evac_eng():
            state["ev"] ^= 1
            return "act" if state["ev"] else "dve"

        def copy_op(eng, out, in_, reads, writes):
            if eng == "act":
                return P.op("act", I("copy", out, in_), reads, writes)
            return P.op(eng, I("tensor_copy", out, in_), reads, writes)

        def wpiece(dram2d, kcn, ncols, key=None):
            s = state["ring"]
            state["ring"] = (s + 1) % RING_N
            view = ring[s][:, 0:kcn * ncols].rearrange("p (k n) -> p k n", k=kcn)
            src = dram2d.rearrange("(k p) n -> p k n", p=128)
            rk = [] if key is None else (list(key) if isinstance(key, list) else [key])
            P.op("sp", I("dma_start", out=view, in_=src), reads=rk, writes=[("ring", s)], dma=True)
            return view, ("ring", s)

        def wpiece_first(src2d, dst2d, kcn, ncols, key):
            i = state["cst"]
            state["cst"] ^= 1
            stgf = ABuf(arena, i * 8192, 1, 4096, F32)
            sview = stgf.ap[:, 0, 0:kcn * ncols].rearrange("p (k n) -> p k n", k=kcn)
            P.op("sp", I("dma_start", out=sview, in_=src2d.rearrange("(k p) n -> p k n", p=128)),
                 writes=stgf.allres(), dma=True)
            s = state["ring"]
            state["ring"] = (s + 1) % RING_N
            view = ring[s][:, 0:kcn * ncols].rearrange("p (k n) -> p k n", k=kcn)
            P.op("act", I("copy", view, sview), reads=stgf.allres(), writes=[("ring", s)])
            P.op("act", I("dma_start", out=dst2d.rearrange("(k p) n -> p k n", p=128), in_=view),
                 reads=[("ring", s)], writes=[key], dma=True)
            return view, ("ring", s)

        def mm(pb, n, lhsT, rhs, start, stop, reads):
            P.op("pe", I("matmul", ps[pb][:, 0:n], lhsT, rhs, start=start, stop=stop),
                 reads=reads, writes=[("ps", pb)])

        for dst, src, key in ((ident, ident_d, "ident"), (identb, identb_d, "identb"), (c128, c128_d, "c128"),
                              (s128n, s128n_d, "s128n"), (vecs, vecs_d, "vecs")):
            P.op("sp", I("dma_start", out=dst[:], in_=src), writes=[key], dma=True)
        P.op("dve", I("memset", ones[:], 1.0 / 1024), writes=["ones"])
        P.op("dve", I("memset", ones512[:], 1.0 / 512), writes=["ones512"])
        P.op("dve", I("memset", ones1[:], 1.0), writes=["ones1"])

        dtmp = ABuf(arena, 0, CW, 128, BF16)
        for c in range(4):
            for t in range(CW):
                col = V_CW + c * CW + t
                P.op("dve", I("tensor_scalar_mul", dtmp.ap[:, t, :], identb[:], vecs[:, col:col + 1]),
                     reads=["identb", "vecs"], writes=dtmp.allres())
            P.op("sp", I("dma_start", out=b_diag[c].rearrange("(t p) n -> p t n", p=128), in_=dtmp.ap),
                 reads=dtmp.allres(), writes=[("b_diag", c)], dma=True)
        def cast(dst, src, key, nsplit=1):
            d2 = dst.rearrange("(p a) n -> p (a n)", p=128)
            s2 = src.rearrange("(p a) n -> p (a n)", p=128)
            step = d2.shape[1] // nsplit
            for i in range(nsplit):
                P.op("pool", I("dma_start", out=d2[:, i * step:(i + 1) * step],
                                                       in_=s2[:, i * step:(i + 1) * step]),
                     reads=[("b_diag", c) for c in range(4)], writes=[(key, i)], dma=True)

        cast(b_in, w_in[0], "b_in")
        cast(b_out, w_out[0], "b_out")
        for l in range(2):
            cast(b_kv[l], wkv[l], ("b_kv", l))
            cast(b_q[l], wq[l], ("b_q", l))
            cast(b_o[l], wo[l], ("b_o", l))
            if l == 0:
                cast(b_pool, w_pool[0].rearrange("g k n -> (g k) n"), "b_pool")
        fence_reads = [("b_in", 0), ("b_out", 0), ("b_pool", 0)]
        for l in range(2):
            fence_reads += [(("b_q", l), 0), (("b_kv", l), 0), (("b_o", l), 0)]
        fence_reads += [("b_diag", c) for c in range(4)]

        P.op("sp", I("dma_start", out=vmask[:], in_=vmask_d[0]), reads=fence_reads, writes=["vmask"], dma=True)

        def load_T(src_rows, nt, dst_ap_fn, dst_res_fn, ev=None):
            s = state["stg"]
            state["stg"] = (s + 1) % 3
            P.op("sp", I("dma_start", out=stg[s][0:nt, :], in_=src_rows), writes=[("stg", s)], dma=True)
            for b in range(2):
                pb = bank()
                for j in range(4):
                    dc = b * 4 + j
                    P.op("pe", I("transpose",
                        ps[pb][:, j * 128:j * 128 + nt], stg[s][0:nt, dc * 128:(dc + 1) * 128], ident[0:nt, 0:nt]),
                        reads=[("stg", s), "ident"], writes=[("ps", pb)])
                src = ps[pb][:].rearrange("p (j n) -> p j n", j=4)[:, :, 0:nt]
                copy_op(ev or evac_eng(), dst_ap_fn(b * 4), src, [("ps", pb)], [dst_res_fn(b * 4 + j) for j in range(4)])

        def rmsnorm(x_ap, x_res, n, gcol, out_ap, out_res, mask_ap=None, mask_res=None, defer=False):
            r = state["rs"]
            state["rs"] ^= 1
            for hf in range(2):
                P.op("act", I("activation", sq[:, hf * 4:(hf + 1) * 4, 0:n], x_ap[:, hf * 4:(hf + 1) * 4, :], AF.Square),
                     reads=x_res[hf * 4:(hf + 1) * 4], writes=[("sq", hf)])
            if defer:
                return lambda: _rms_rest(x_ap, x_res, n, gcol, out_ap, out_res, mask_ap, mask_res, r)
            _rms_rest(x_ap, x_res, n, gcol, out_ap, out_res, mask_ap, mask_res, r)

        def _rms_rest(x_ap, x_res, n, gcol, out_ap, out_res, mask_ap, mask_res, r):
            pb = bank()
            for dc in range(KC):
                mm(pb, n, ones[:], sq[:, dc, 0:n], dc == 0, dc == KC - 1, [("sq", dc // 4), "ones"])
            P.op("act", I("activation", rstd[r][:, 0:n], ps[pb][:, 0:n], AF.Ln, bias=RMS_EPS),
                 reads=[("ps", pb)], writes=[("rstd", r)])
            P.op("act", I("activation", rstd[r][:, 0:n], rstd[r][:, 0:n], AF.Exp, scale=-0.5),
                 reads=[("rstd", r)], writes=[("rstd", r)])
            if mask_ap is not None:
                P.op("dve", I("tensor_mul", rstd[r][:, 0:n], rstd[r][:, 0:n], mask_ap),
                     reads=[("rstd", r), mask_res], writes=[("rstd", r)])
            for dc in range(KC):
                P.op("dve", I("scalar_tensor_tensor",
                    out_ap[:, dc, :], x_ap[:, dc, :], vecs[:, gcol + dc:gcol + dc + 1], rstd[r][:, 0:n],
                    ALU.mult, ALU.mult),
                    reads=[x_res[dc], ("rstd", r), "vecs"], writes=[out_res[dc]])

        def xres(ti):
            return [("xT", dc, ti) for dc in range(KC)]

        def hres(hi):
            return [("hT", hi, dc) for dc in range(KC)]

        def norm_tile(ti, gcol, mask_ap=None, mask_res=None):
            c0, n = TILES[ti]
            rmsnorm(xT[:, :, c0:c0 + n], xres(ti), n, gcol, hbuf[ti][:, :, 0:n], hres(ti), mask_ap, mask_res)

        def resid_add(ti, mc, pb, scale_col=None):
            c0, n = TILES[ti]
            dst = xT[:, mc, c0:c0 + n]
            if scale_col is None:
                P.op("dve", I("tensor_add", dst, dst, ps[pb][:, 0:n]),
                     reads=[("ps", pb), ("xT", mc, ti)], writes=[("xT", mc, ti)])
            else:
                P.op("dve", I("scalar_tensor_tensor", dst, ps[pb][:, 0:n], vecs[:, scale_col:scale_col + 1],
                                                            dst, ALU.mult, ALU.add),
                     reads=[("ps", pb), ("xT", mc, ti), "vecs"], writes=[("xT", mc, ti)])

        def proj_resid(wdram, key, rhs_fn, tiles):
            for ti in tiles:
                n = TILES[ti][1]
                for half in range(2):
                    wv, wr = wpiece(wdram[:, half * 512:(half + 1) * 512], KC, 512, key)
                    for m4 in range(4):
                        pb = bank()
                        for kc in range(KC):
                            ra, rr = rhs_fn(ti, kc)
                            mm(pb, n, wv[:, kc, m4 * 128:(m4 + 1) * 128], ra, kc == 0, kc == KC - 1, [wr, rr])
                        resid_add(ti, half * 4 + m4, pb)

        def attn_bufs():
            return dict(q=ABuf(arena, 0, 8, 512, BF16), o=ABuf(arena, 4096, 8, 512, BF16),
                        Pt=ABuf(arena, 8192, 4, 512, BF16), KT=ABuf(arena, 10240, 8, 256, BF16),
                        V=ABuf(arena, 12288, 2, 1024, BF16), rden=ABuf(arena, 14336, 2, 512, F32),
                        mn=ABuf(arena, 18432, 8, 256, BF16), memT=ABuf(arena, 20480, 8, 256, F32))

        def attn_kv_load(seq):
            memT = attn_bufs()["memT"]
            for j in range(2):
                load_T(mem[seq][j * 128:(j + 1) * 128], 128,
                       lambda dc0, j=j: memT.ap[:, dc0:dc0 + 4, j * 128:(j + 1) * 128],
                       lambda dc: memT.res(dc), ev="act")

        def attn_kv(l, seq, is_a):
            B = attn_bufs()
            memT, mn, KT, V = B["memT"], B["mn"], B["KT"], B["V"]
            kvap = arena[:, KT.off:V.end]
            kvres = KT.allres() + V.allres()
            if not is_a:
                P.op("sp", I("dma_start", out=kvap, in_=kv_scr[seq][l]), reads=[("kv_scr", seq, l)], writes=kvres, dma=True)
                return
            attn_kv_load(seq)
            rmsnorm(memT.ap[:, :, :], [memT.res(dc) for dc in range(KC)], 256, V_NMEM + l * 8,
                    mn.ap[:, :, :], [mn.res(dc) for dc in range(KC)])
            for half in range(2):
                wv, wr = wpiece(b_kv[l][:, half * 512:(half + 1) * 512], KC, 512, (("b_kv", l), 0))
                for m4 in range(4):
                    oc = half * 4 + m4
                    pb = bank()
                    for kc in range(KC):
                        mm(pb, 256, wv[:, kc, m4 * 128:(m4 + 1) * 128], mn.ap[:, kc, :], kc == 0, kc == KC - 1,
                           [wr, mn.res(kc)])
                    copy_op("act", KT.ap[:, oc, :], ps[pb][:, 0:256], [("ps", pb)], [KT.res(oc)])
            for half in range(2):
                wv, wr = wpiece(b_kv[l][:, 1024 + half * 512:1024 + (half + 1) * 512], KC, 512, (("b_kv", l), 0))
                for j in range(2):
                    pb = bank()
                    for kc in range(KC):
                        mm(pb, 512, mn.ap[:, kc, j * 128:(j + 1) * 128], wv[:, kc, :], kc == 0, kc == KC - 1,
                           [wr, mn.res(kc)])
                    copy_op("act", V.ap[:, j, half * 512:(half + 1) * 512], ps[pb][:, :], [("ps", pb)],
                            [V.res(j, half * 512, (half + 1) * 512)])
            P.op("sp", I("dma_start", out=kv_scr[seq][l], in_=kvap), reads=kvres, writes=[("kv_scr", seq, l)], dma=True)

        def attention(l, seq, tiles, is_a, kv_done=False):
            B = attn_bufs()
            q, o, Pt, KT, V, rden = B["q"], B["o"], B["Pt"], B["KT"], B["V"], B["rden"]
            pti = [0]

            def qproj(ti):
                n = TILES[ti][1]
                for half in range(2):
                    wv, wr = wpiece(b_q[l][:, half * 512:(half + 1) * 512], KC, 512, (("b_q", l), 0))
                    for m4 in range(4):
                        mc = half * 4 + m4
                        pb = bank()
                        for kc in range(KC):
                            mm(pb, n, wv[:, kc, m4 * 128:(m4 + 1) * 128], hbuf[ti][:, kc, 0:n], kc == 0, kc == KC - 1,
                               [wr, ("hT", ti, kc)])
                        copy_op(evac_eng(), q.ap[:, mc, 0:n], ps[pb][:, 0:n], [("ps", pb)], [q.res(mc)])

            def heads(ti):
                n = TILES[ti][1]

                def scores(h, pbuf):
                    for j in range(2):
                        pb = bank()
                        for dd in range(2):
                            dc = 2 * h + dd
                            mm(pb, n, KT.ap[:, dc, j * 128:(j + 1) * 128], q.ap[:, dc, 0:n], dd == 0, dd == 1,
                               [KT.res(dc), q.res(dc)])
                        P.op("act", I("activation", Pt.ap[:, pbuf * 2 + j, 0:n], ps[pb][:, 0:n], AF.Exp,
                                      scale=float(XA_SCALE)),
                             reads=[("ps", pb)], writes=[Pt.res(pbuf * 2 + j)])

                def rest(h, pbuf):
                    pb = bank()
                    for j in range(2):
                        mm(pb, n, ones1[:], Pt.ap[:, pbuf * 2 + j, 0:n], j == 0, j == 1, ["ones1", Pt.res(pbuf * 2 + j)])
                    rd = rden.ap[:, pbuf, 0:n]
                    P.op("act", I("activation", rd, ps[pb][:, 0:n], AF.Ln), reads=[("ps", pb)], writes=[rden.res(pbuf)])
                    P.op("act", I("activation", rd, rd, AF.Exp, scale=-1.0), reads=[rden.res(pbuf)], writes=[rden.res(pbuf)])
                    for dd in range(2):
                        dc = 2 * h + dd
                        pb = bank()
                        for j in range(2):
                            mm(pb, n, V.ap[:, j, dc * 128:(dc + 1) * 128], Pt.ap[:, pbuf * 2 + j, 0:n], j == 0, j == 1,
                               [V.res(j), Pt.res(pbuf * 2 + j)])
                        P.op("dve", I("tensor_mul", o.ap[:, dc, 0:n], ps[pb][:, 0:n], rd),
                             reads=[("ps", pb), rden.res(pbuf)], writes=[o.res(dc)])

                pbs_ = [(pti[0] + h) % 2 for h in range(4)]
                pti[0] += 4
                scores(0, pbs_[0])
                for h in range(4):
                    if h + 1 < 4:
                        scores(h + 1, pbs_[h + 1])
                    rest(h, pbs_[h])

            def oproj(ti):
                proj_resid(b_o[l], (("b_o", l), 0), lambda ti_, kc: (o.ap[:, kc, 0:TILES[ti_][1]], o.res(kc)), [ti])

            norm_tile(tiles[0], V_NXA + l * 8)
            qproj(tiles[0])
            if not kv_done:
                attn_kv(l, seq, is_a)
            for idx, ti in enumerate(tiles):
                nxt = tiles[idx + 1] if idx + 1 < len(tiles) else None
                heads(ti)
                if nxt is not None:
                    norm_tile(nxt, V_NXA + l * 8)
                oproj(ti)
                if nxt is not None:
                    qproj(nxt)

        def ffn(l, tiles, first=False):
            sg = ABuf(arena, 16384, 2, 512, F32)
            act = ABuf(arena, 20480, FC, XC, BF16)
            assert act.end <= AREN
            for ti in tiles:
                norm_tile(ti, V_NFFN + l * 8)
            sgi = 0
            for pc in range(6):
                ncol = 512 if pc < 5 else 256
                if first:
                    gv, gr = wpiece_first(wgu[l][:, pc * 512:pc * 512 + ncol], b_gu[l][:, pc * 512:pc * 512 + ncol],
                                          KC, ncol, ("b_gu", l, pc, 0))
                    uv, ur = wpiece_first(wgu[l][:, DFF + pc * 512:DFF + pc * 512 + ncol],
                                          b_gu[l][:, DFF + pc * 512:DFF + pc * 512 + ncol], KC, ncol, ("b_gu", l, pc, 1))
                else:
                    gv, gr = wpiece(b_gu[l][:, pc * 512:pc * 512 + ncol], KC, ncol, ("b_gu", l, pc, 0))
                    uv, ur = wpiece(b_gu[l][:, DFF + pc * 512:DFF + pc * 512 + ncol], KC, ncol, ("b_gu", l, pc, 1))
                for ti in tiles:
                    c0, n = TILES[ti]
                    for jj in range(ncol // 128):
                        ch = pc * 4 + jj
                        pg, pu = bank(), bank()
                        for kc in range(KC):
                            mm(pg, n, gv[:, kc, jj * 128:(jj + 1) * 128], hbuf[ti][:, kc, 0:n], kc == 0, kc == KC - 1,
                               [gr, ("hT", ti, kc)])
                        for kc in range(KC):
                            mm(pu, n, uv[:, kc, jj * 128:(jj + 1) * 128], hbuf[ti][:, kc, 0:n], kc == 0, kc == KC - 1,
                               [ur, ("hT", ti, kc)])
                        sb = sgi % 2
                        sgi += 1
                        sgd = sg.ap[:, sb, 0:n]
                        P.op("act", I("activation", sgd, ps[pg][:, 0:n], AF.Silu),
                             reads=[("ps", pg)], writes=[sg.res(sb)])
                        ad = act.ap[:, ch, c0:c0 + n]
                        P.op("dve", I("tensor_mul", ad, ps[pu][:, 0:n], sgd),
                             reads=[("ps", pu), sg.res(sb)], writes=[act.res(ch, c0, c0 + n)])
            for m2 in range(4):
                if first:
                    wa, war = wpiece_first(wdn[l][0:1408, m2 * 256:(m2 + 1) * 256], b_dn[l][0:1408, m2 * 256:(m2 + 1) * 256],
                                           11, 256, ("b_dn", l, m2, 0))
                    wb, wbr = wpiece_first(wdn[l][1408:2816, m2 * 256:(m2 + 1) * 256],
                                           b_dn[l][1408:2816, m2 * 256:(m2 + 1) * 256], 11, 256, ("b_dn", l, m2, 1))
                else:
                    wa, war = wpiece(b_dn[l][0:1408, m2 * 256:(m2 + 1) * 256], 11, 256, ("b_dn", l, m2, 0))
                    wb, wbr = wpiece(b_dn[l][1408:2816, m2 * 256:(m2 + 1) * 256], 11, 256, ("b_dn", l, m2, 1))
                for ti in tiles:
                    c0, n = TILES[ti]
                    for mm_ in range(2):
                        pb = bank()
                        for kc in range(FC):
                            wv, wr = (wa, war) if kc < 11 else (wb, wbr)
                            mm(pb, n, wv[:, kc % 11, mm_ * 128:(mm_ + 1) * 128], act.ap[:, kc, c0:c0 + n],
                               kc == 0, kc == FC - 1, [wr, act.res(kc, c0, c0 + n)])
                        resid_add(ti, m2 * 2 + mm_, pb)

        out_ops = []
        for u in range(NUNIT):
            is_p = u < 2
            nsc = 32 if is_p else 16
            seq = 0 if is_p else 1 + (u - 2) // 2
            tab = tabP[u] if is_p else tabS[(u - 2) % 2]
            others = []
            if u % 2 == 0:
                others = [xu[u + 1][i * 512:(i + 1) * 512] for i in range(2)]
                if is_p:
                    others += [xop[i * 512:(i + 1) * 512] for i in range(4)]

            P.op("sp", I("dma_start", out=corr[:].rearrange("p a b -> p (a b)"), in_=corr_d[u]),
                 writes=["corr"], dma=True)
            P.op("sp", I("dma_start", out=vmask[:], in_=vmask_d[u]), writes=["vmask"], dma=True)

            Utm = ABuf(arena, 0, nsc, 512, BF16)
            hglu = ABuf(arena, 16384, 4, 1088, BF16)
            hc = ABuf(arena, hglu.end, 4, 512, F32)
            hcb = ABuf(arena, hc.end, 4, 512, BF16)
            hsq = ABuf(arena, hcb.end, 4, 512, BF16)
            xo = ABuf(arena, hc.off, 8, 512, F32)
            assert xo.end == hsq.end
            cat = ABuf(arena, hsq.end, 8, 512, BF16)
            ABt = [ABuf(arena, cat.end, 8, 512, BF16), ABuf(arena, cat.end + 4096, 8, 512, BF16)]
            lnm = ABuf(arena, ABt[0].off, 3, 512, F32)
            sgt = ABuf(arena, ABt[0].off, 1, 512, F32)
            adt = ABuf(arena, ABt[1].off, 1, 512, BF16)
            assert ABt[1].end <= AREN, ABt[1].end

            P.stage = "load"
            for tcn in range(8):
                ti = tcn // 4
                load_T(xu[u][tcn * 128:(tcn + 1) * 128], 128,
                       lambda dc0, tcn=tcn: xT[:, dc0:dc0 + 4, tcn * 128:(tcn + 1) * 128],
                       lambda dc, ti=ti: ("xT", dc, ti))
            load_T(xh[u], 64, lambda dc0: xhT[:, dc0:dc0 + 4, :], lambda dc: ("xhT", dc))
            for dc0 in (0, 4):
                copy_op("dve", xT[:, dc0:dc0 + 4, UT:XC], xhT[:, dc0:dc0 + 4, 16:48],
                        [("xhT", dc0 + j) for j in range(4)], [("xT", dc0 + j, 2) for j in range(4)])

            P.stage = "inproj"
            def glu(valw, gatew, hb, hi, n, dsts):
                for c in range(4):
                    pv, pg = bank(), bank()
                    for (pb, (wv, wr)) in ((pv, valw), (pg, gatew)):
                        for kc in range(KC):
                            mm(pb, n, wv[:, kc, c * 128:(c + 1) * 128], hb[:, kc, 0:n], kc == 0, kc == KC - 1,
                               [wr, ("hT", hi, kc)])
                    P.op("act", I("activation", sgt.ap[:, 0, 0:n], ps[pg][:, 0:n], AF.Sigmoid),
                         reads=[("ps", pg)], writes=sgt.allres())
                    for (d0, s0, w_) in dsts:
                        dstg = hglu.ap[:, c, d0:d0 + w_]
                        P.op("dve", I("tensor_mul",
                            dstg, ps[pv][:, s0:s0 + w_], sgt.ap[:, 0, s0:s0 + w_]),
                            reads=[("ps", pv)] + sgt.allres(), writes=[hglu.res(c, d0, d0 + w_)])

            def uproj(fw_, hb, hi, sc0):
                wv, wr = fw_
                for tcn in range(4):
                    pb = bank()
                    for kc in range(KC):
                        mm(pb, 512, hb[:, kc, tcn * 128:(tcn + 1) * 128], wv[:, kc, :], kc == 0, kc == KC - 1,
                           [wr, ("hT", hi, kc)])
                    copy_op(evac_eng(), Utm.ap[:, sc0 + tcn, :], ps[pb][:, :], [("ps", pb)], [Utm.res(sc0 + tcn)])

            is_a = (u % 2 == 0)
            useq = u_scr[seq][0:nsc * 128].rearrange("(k p) n -> p k n", p=128)
            xo2 = ABuf(arena, hsq.end, 8, 512, F32)
            norm_tile(0, V_NMIX + 0)
            norm_tile(1, V_NMIX + 0)
            P.stage = "inproj_halo"
            rmsnorm(xhT[:, :, :], [("xhT", dc) for dc in range(KC)], 64, V_NMIX + 0, hbuf[2][:, :, 0:64], hres(2))
            P.stage = "inproj"
            if not is_a:
                P.op("sp", I("dma_start", out=Utm.ap, in_=useq), reads=[("u_scr", seq)], writes=Utm.allres(), dma=True)
            for ti in range(2):
                valw = wpiece(b_in[:, 0:512], KC, 512, ("b_in", 0))
                gatew = wpiece(b_in[:, 512:1024], KC, 512, ("b_in", 0))
                glu(valw, gatew, hbuf[ti], ti, 512, [(32 + ti * 512, 0, 512)])
                if is_a:
                    fw_ = wpiece(b_in[:, 1024:1536], KC, 512, ("b_in", 0))
                    uproj(fw_, hbuf[ti], ti, ti * 4)
            P.stage = "inproj_halo"
            valw = wpiece(b_in[:, 0:512], KC, 512, ("b_in", 0))
            gatew = wpiece(b_in[:, 512:1024], KC, 512, ("b_in", 0))
            glu(valw, gatew, hbuf[2], 2, 64, [(0, 0, 32), (32 + UT, 32, 32)])
            P.stage = "others"
            if is_a:
                xobufs = [xo, xo2]

                def load_other(oi):
                    xb = xobufs[oi % 2]
                    for tcn in range(4):
                        load_T(others[oi][tcn * 128:(tcn + 1) * 128], 128,
                               lambda dc0, tcn=tcn, xb=xb: xb.ap[:, dc0:dc0 + 4, tcn * 128:(tcn + 1) * 128],
                               lambda dc, xb=xb: xb.res(dc))

                def norm_other(oi, defer=False):
                    xb = xobufs[oi % 2]
                    return rmsnorm(xb.ap[:, :, :], [xb.res(dc) for dc in range(KC)], 512, V_NMIX + 0,
                                   hbuf[oi % 2][:, :, :], hres(oi % 2), defer=defer)

                load_other(0)
                norm_other(0)
                for oi in range(len(others)):
                    if oi + 1 < len(others):
                        load_other(oi + 1)
                    fw_ = wpiece(b_in[:, 1024:1536], KC, 512, ("b_in", 0))
                    fin = norm_other(oi + 1, defer=True) if oi + 1 < len(others) else None
                    uproj(fw_, hbuf[oi % 2], oi % 2, 8 + oi * 4)
                    if fin is not None:
                        fin()
                    hh = nsc // 2
                    for hi_c in range(8 + oi * 4, 12 + oi * 4):
                        sc = hi_c - hh
                        if sc < 0:
                            continue
                        P.op("dve", I("tensor_add", adt.ap[:, 0, :], Utm.ap[:, sc, :], Utm.ap[:, sc + hh, :]),
                             reads=[Utm.res(sc), Utm.res(sc + hh)], writes=adt.allres())
                        P.op("dve", I("tensor_sub", Utm.ap[:, sc + hh, :], Utm.ap[:, sc, :], Utm.ap[:, sc + hh, :]),
                             reads=[Utm.res(sc), Utm.res(sc + hh)], writes=[Utm.res(sc + hh)])
                        P.op("dve", I("tensor_copy", Utm.ap[:, sc, :], adt.ap[:, 0, :]),
                             reads=adt.allres(), writes=[Utm.res(sc)])
                P.op("sp", I("dma_start", out=useq, in_=Utm.ap), reads=Utm.allres(), writes=[("u_scr", seq)], dma=True)

            for ti in range(3):
                c0, n = TILES[ti]
                P.stage = "conv%d" % ti
                if ti < 2:
                    segs = [(0, c0 + 17, n)]
                else:
                    segs = [(0, 1, 16), (16, 1024 + 17, 16)]
                for c in range(4):
                    dg, dgr = wpiece(b_diag[c], CW, 128, ("b_diag", c))
                    pb = bank()
                    for (p0, h0, w_) in segs:
                        for t in range(CW):
                            P.op("pe", I("matmul",
                                ps[pb][:, p0:p0 + w_], dg[:, t, :], hglu.ap[:, c, h0 + t:h0 + t + w_],
                                start=(t == 0), stop=(t == CW - 1)),
                                reads=[dgr, hglu.res(c)], writes=[("ps", pb)])
                    P.op("act", I("activation", hc.ap[:, c, 0:n], ps[pb][:, 0:n], AF.Identity,
                                                                  bias=vecs[:, V_CB + c:V_CB + c + 1]),
                         reads=[("ps", pb), "vecs"], writes=[hc.res(c)])
                    P.op("act", I("activation", hsq.ap[:, c, 0:n], ps[pb][:, 0:n], AF.Square,
                                                                  bias=vecs[:, V_CB + c:V_CB + c + 1]),
                         reads=[("ps", pb), "vecs"], writes=[hsq.res(c)])
                    copy_op("dve", hcb.ap[:, c, 0:n], hc.ap[:, c, 0:n], [hc.res(c)], [hcb.res(c)])
                pm, pq = bank(), bank()
                for c in range(4):
                    mm(pm, n, ones512[:], hcb.ap[:, c, 0:n], c == 0, c == 3, ["ones512", hcb.res(c)])
                for c in range(4):
                    mm(pq, n, ones512[:], hsq.ap[:, c, 0:n], c == 0, c == 3, ["ones512", hsq.res(c)])
                mean, rs_, tmp = lnm.ap[:, 0, 0:n], lnm.ap[:, 1, 0:n], lnm.ap[:, 2, 0:n]
                copy_op("dve", mean, ps[pm][:, 0:n], [("ps", pm)], [lnm.res(0)])
                P.op("dve", I("tensor_mul", tmp, mean, mean), reads=[lnm.res(0)], writes=[lnm.res(2)])
                P.op("dve", I("tensor_sub", rs_, ps[pq][:, 0:n], tmp),
                     reads=[("ps", pq), lnm.res(2)], writes=[lnm.res(1)])
                P.op("act", I("activation", rs_, rs_, AF.Ln, bias=LN_EPS),
                     reads=[lnm.res(1)], writes=[lnm.res(1)])
                P.op("act", I("activation", rs_, rs_, AF.Exp, scale=-0.5), reads=[lnm.res(1)], writes=[lnm.res(1)])
                for c in range(4):
                    hcc = hc.ap[:, c, 0:n]
                    P.op("dve", I("tensor_sub", hcc, hcc, mean),
                         reads=[hc.res(c), lnm.res(0)], writes=[hc.res(c)])
                    P.op("dve", I("tensor_mul", hcc, hcc, rs_),
                         reads=[hc.res(c), lnm.res(1)], writes=[hc.res(c)])
                    P.op("act", I("activation",
                        cat.ap[:, c, 0:n], hcc, AF.Silu, bias=vecs[:, V_LB + c:V_LB + c + 1],
                        scale=vecs[:, V_LG + c:V_LG + c + 1]),
                        reads=[hc.res(c), "vecs"], writes=[cat.res(c)])
                P.stage = "dft%d" % ti
                if ti != 1:
                    hh = nsc // 2
                    tcol, gw = (0, 512) if ti == 0 else (1024, 16)
                    for cs in range(2):
                        for par in range(2):
                            pbs = [bank() for _ in range(4)]
                            for pi in range(hh // 8):
                                if ti == 0:
                                    tv, tr = wpiece(tab[cs][pi * 1024:(pi + 1) * 1024, par * 512:(par + 1) * 512], 8, 512)
                                    tsl = lambda s8: tv[:, s8, :]
                                else:
                                    tv, tr = wpiece(tab[cs][pi * 1024:(pi + 1) * 1024, 1024:1056], 8, 32)
                                    tsl = lambda s8, tv=tv, par=par: tv[:, s8, par * 16:(par + 1) * 16]
                                for s8 in range(8):
                                    sc = pi * 8 + s8
                                    src = sc + par * hh
                                    for c in range(4):
                                        mm(pbs[c], gw, Utm.ap[:, src, c * 128:(c + 1) * 128], tsl(s8), sc == 0, sc == hh - 1,
                                           [tr, Utm.res(src)])
                            for c in range(4):
                                if ti == 0:
                                    dst2 = arena[:, ABt[0].off:ABt[0].off + 8192].rearrange(
                                        "p (t r c) -> p t r c", t=2, r=8)[:, :, cs * 4 + c, par:512:2]
                                    src2 = ps[pbs[c]][:, :].rearrange("p (t c) -> p t c", t=2)
                                    copy_op(evac_eng(), dst2, src2, [("ps", pbs[c])],
                                            [ABt[0].res(cs * 4 + c), ABt[1].res(cs * 4 + c)])
                                else:
                                    copy_op(evac_eng(), ABt[0].ap[:, cs * 4 + c, par:32:2], ps[pbs[c]][:, 0:16],
                                            [("ps", pbs[c])], [ABt[0].res(cs * 4 + c)])
                AB = ABt[ti % 2]
                for g in range(4):
                    pb = bank()
                    mm(pb, n, c128[:], AB.ap[:, g, 0:n], True, False, ["c128", AB.res(g)])
                    mm(pb, n, s128n[:], AB.ap[:, 4 + g, 0:n], False, True, ["s128n", AB.res(4 + g)])
                    copy_op(evac_eng(), cat.ap[:, 4 + g, 0:n], ps[pb][:, 0:n], [("ps", pb)], [cat.res(4 + g)])
                proj_resid(b_out, ("b_out", 0), lambda ti_, kc: (cat.ap[:, kc, 0:TILES[ti_][1]], cat.res(kc)), [ti])

            P.stage = "attn0"
            attention(0, seq, [0, 1, 2], is_a)
            P.stage = "ffn0"
            ffn(0, [0, 1, 2], first=(u == 0))

            P.stage = "pool"
            hpad = ABuf(arena, 0, 8, XC, BF16)
            for ti in range(2):
                c0, n = TILES[ti]
                rmsnorm(xT[:, :, c0:c0 + n], xres(ti), n, V_NMIX + 8, hpad.ap[:, :, 16 + c0:16 + c0 + n],
                        [hpad.res(dc, 16 + c0, 16 + c0 + n) for dc in range(KC)])
            norm_tile(2, V_NMIX + 8, vmask[:, :], "vmask")
            for dc0 in (0, 4):
                copy_op("dve", hpad.ap[:, dc0:dc0 + 4, 0:16], hbuf[2][:, dc0:dc0 + 4, 0:16],
                        [("hT", 2, dc0 + j) for j in range(4)], [hpad.res(dc0 + j, 0, 16) for j in range(4)])
                copy_op("dve", hpad.ap[:, dc0:dc0 + 4, 16 + UT:XC], hbuf[2][:, dc0:dc0 + 4, 16:32],
                        [("hT", 2, dc0 + j) for j in range(4)], [hpad.res(dc0 + j, 16 + UT, XC) for j in range(4)])
            attn_kv(1, seq, is_a)
            pbf = ABuf(arena, 24576, 8, UT, BF16)
            te = ABuf(arena, pbf.end, 2, 8, F32)
            tei = 0
            for ti in range(2):
                c0, n = TILES[ti]
                for dc in range(KC):
                    w_ = 2 << (dc // 2)
                    base = 16 + c0 - w_ // 2
                    pb = bank()
                    for j in range(w_):
                        mm(pb, n, identb[:], hpad.ap[:, dc, base + j:base + j + n], j == 0, j == w_ - 1,
                           ["identb", hpad.res(dc)])
                    P.op("dve", I("scalar_tensor_tensor", pbf.ap[:, dc, c0:c0 + n], ps[pb][:, 0:n], 1.0 / w_,
                                  hpad.ap[:, dc, 16 + c0:16 + c0 + n], ALU.mult, ALU.subtract),
                         reads=[("ps", pb), hpad.res(dc)], writes=[pbf.res(dc, c0, c0 + n)])
                    e0, k0 = (0, 0) if ti == 0 else (n - 8, 8)
                    tea = te.ap[:, tei % 2, :]
                    ter = te.res(tei % 2)
                    tei += 1
                    P.op("dve", I("tensor_mul", tea, ps[pb][:, e0:e0 + 8], corr[:, dc, k0:k0 + 8]),
                         reads=[("ps", pb), "corr"], writes=[ter])
                    P.op("dve", I("tensor_sub", pbf.ap[:, dc, c0 + e0:c0 + e0 + 8], tea,
                                  hpad.ap[:, dc, 16 + c0 + e0:16 + c0 + e0 + 8]),
                         reads=[ter, hpad.res(dc), pbf.res(dc, c0, c0 + n)], writes=[pbf.res(dc, c0, c0 + n)])
            pw, pwr = wpiece(b_pool, 8, 256, ("b_pool", 0))
            for ti in range(2):
                c0, n = TILES[ti]
                for g in range(4):
                    for oc in range(2):
                        pb = bank()
                        for kc in range(2):
                            mm(pb, n, pw[:, g * 2 + kc, oc * 128:(oc + 1) * 128], pbf.ap[:, g * 2 + kc, c0:c0 + n],
                               kc == 0, kc == 1, [pwr, pbf.res(g * 2 + kc, c0, c0 + n)])
                        resid_add(ti, g * 2 + oc, pb, V_PSC + g * 2 + oc)

            P.stage = "attn1"
            attention(1, seq, [0, 1], is_a, kv_done=True)
            P.stage = "ffn1"
            ffn(1, [0, 1], first=(u == 0))

            P.stage = "final"
            yT = ABuf(arena, 0, 8, UT, F32)
            for ti in range(2):
                c0, n = TILES[ti]
                rmsnorm(xT[:, :, c0:c0 + n], xres(ti), n, V_NFIN, yT.ap[:, :, c0:c0 + n],
                        [yT.res(dc, c0, c0 + n) for dc in range(KC)])
            for tcn in range(8):
                ti = tcn // 4
                s = state["ostg"]
                state["ostg"] = (s + 1) % 2
                for b in range(2):
                    pb = bank()
                    for j in range(4):
                        dc = b * 4 + j
                        P.op("pe", I("transpose",
                            ps[pb][:, j * 128:(j + 1) * 128], yT.ap[:, dc, tcn * 128:(tcn + 1) * 128], ident[:]),
                            reads=[yT.res(dc, tcn * 128, (tcn + 1) * 128), "ident"], writes=[("ps", pb)])
                    copy_op(evac_eng(), ostg[s][:, b * 512:(b + 1) * 512], ps[pb][:, :], [("ps", pb)], [("ostg", s, b)])
                out_ops.append(P.op("act", I("dma_start",
                    out=yu[u][tcn * 128:(tcn + 1) * 128, :], in_=ostg[s][:, :]), reads=[("ostg", s, 0), ("ostg", s, 1)], dma=True))

        P.emit(nc, final_wait_ops=out_ops)
        _PROG["P"] = P
    return nc


_BF = ml_dtypes.bfloat16
_CACHE = {}


def _tables(S, s0, others_pos):
    rows = np.concatenate(others_pos).astype(np.int64)[:S // 2]
    own = np.arange(s0, s0 + UT)
    halo = np.concatenate([np.arange(s0 - HAL, s0), np.arange(s0 + UT, s0 + UT + HAL)])
    cols = np.concatenate([own[0::2], own[1::2], halo[0::2], halo[1::2]]).astype(np.int64)
    ph = (rows[:, None] * cols[None, :]) % S
    ang = ph.astype(np.float64) * (2.0 * np.pi / S)
    sc = 1.0 / np.sqrt(S * 128.0)
    return np.stack([(np.cos(ang) * sc).astype(_BF), (np.sin(ang) * sc).astype(_BF)])


def _consts():
    if "c" in _CACHE:
        return _CACHE["c"]
    c = {}
    c["tabP"] = {}
    for hf in range(2):
        t = []
        for u in range(2):
            s0 = hf * 2048 + u * UT
            oth = (1 - hf) * 2048
            t.append(_tables(4096, s0, [np.arange(hf * 2048, hf * 2048 + 2048), np.arange(oth, oth + 2048)]))
        c["tabP"][hf] = np.ascontiguousarray(np.stack(t))
    t = []
    for pos in range(2):
        s0 = pos * UT
        t.append(_tables(2048, s0, [np.arange(0, 2048)]))
    c["tabS"] = np.ascontiguousarray(np.stack(t))
    j = np.arange(128, dtype=np.int64)
    ang = ((j[:, None] * j[None, :]) % 128).astype(np.float64) * (2.0 * np.pi / 128)
    c["c128"] = np.cos(ang).astype(_BF)
    c["s128n"] = (-np.sin(ang)).astype(_BF)
    c["ident"] = np.eye(128, dtype=np.float32)
    c["identb"] = np.eye(128, dtype=np.float32).astype(_BF)
    _CACHE["c"] = c
    return c


def _unit_meta(S, s0):
    corr = np.zeros((KC, 16), np.float32)
    for dc in range(KC):
        w = 2 << (dc // 2)
        for i in range(8):
            for (col, p) in ((i, s0 + i), (8 + i, s0 + UT - 8 + i)):
                lo = max(p - w // 2, 0)
                hi = min(p + w - w // 2, S)
                corr[dc, col] = 1.0 / float(hi - lo)
    vm = np.zeros((32,), np.float32)
    vm[0:16] = 1.0 if s0 > 0 else 0.0
    vm[16:32] = 1.0 if s0 + UT < S else 0.0
    return corr, vm


def _halo(xs, s0):
    S = xs.shape[0]
    out = np.zeros((64, D), np.float32)
    lo = s0 - 32
    if lo >= 0:
        out[0:32] = xs[lo:s0]
    hi = s0 + UT
    if hi + 32 <= S:
        out[32:64] = xs[hi:hi + 32]
    return out


def _fm(v, nch):
    return np.ascontiguousarray(np.asarray(v, np.float32).reshape(nch, 128).T)


def kernel(x_prompt, x_sample, mem_prompt, mem_sample, norm_mix, w_in_even, conv_w, conv_b, conv_ln_g, conv_ln_b,
           w_out_even, w_pool, pool_scale, norm_xa, norm_mem, xa_wq, xa_wkv, xa_wo, norm_ffn, ffn_w_gate_up,
           ffn_w_down, norm_final):
    f32 = lambda a: np.ascontiguousarray(np.asarray(a, dtype=np.float32))
    x_prompt, x_sample, mem_prompt, mem_sample = f32(x_prompt), f32(x_sample), f32(mem_prompt), f32(mem_sample)
    cst = _consts()
    vecs = np.zeros((128, NVEC), np.float32)
    for l in range(2):
        vecs[:, V_NMIX + l * 8:V_NMIX + l * 8 + 8] = _fm(norm_mix[l], 8)
        vecs[:, V_NXA + l * 8:V_NXA + l * 8 + 8] = _fm(norm_xa[l], 8)
        vecs[:, V_NMEM + l * 8:V_NMEM + l * 8 + 8] = _fm(norm_mem[l], 8)
        vecs[:, V_NFFN + l * 8:V_NFFN + l * 8 + 8] = _fm(norm_ffn[l], 8)
    vecs[:, V_NFIN:V_NFIN + 8] = _fm(norm_final, 8)
    vecs[:, V_PSC:V_PSC + 8] = _fm(np.asarray(pool_scale)[0], 8)
    vecs[:, V_CB:V_CB + 4] = _fm(np.asarray(conv_b)[0], 4)
    vecs[:, V_LG:V_LG + 4] = _fm(np.asarray(conv_ln_g)[0], 4)
    vecs[:, V_LB:V_LB + 4] = _fm(np.asarray(conv_ln_b)[0], 4)
    cw = np.asarray(conv_w, np.float32)[0]
    for c in range(4):
        vecs[:, V_CW + c * CW:V_CW + (c + 1) * CW] = cw[:, c * 128:(c + 1) * 128].T
    shared = {
        "vecs": vecs, "ident": cst["ident"], "identb": cst["identb"], "c128": cst["c128"], "s128n": cst["s128n"],
        "tabS": cst["tabS"],
        "w_in_even": f32(w_in_even), "w_out_even": f32(w_out_even), "w_pool": f32(w_pool), "xa_wq": f32(xa_wq),
        "xa_wkv": f32(xa_wkv), "xa_wo": f32(xa_wo), "ffn_w_gate_up": f32(ffn_w_gate_up), "ffn_w_down": f32(ffn_w_down),
    }
    in_maps = []
    for c in range(NCORES):
        pi, hf = c // 2, c % 2
        xu = np.empty((NUNIT, UT, D), np.float32)
        xh = np.empty((NUNIT, 64, D), np.float32)
        corr = np.empty((NUNIT, 128, 128), np.float32)
        vmask = np.empty((NUNIT, 128, 32), np.float32)
        for u in range(NUNIT):
            if u < 2:
                xs, S, s0 = x_prompt[pi], 4096, hf * 2048 + u * UT
            else:
                xs, S, s0 = x_sample[4 * c + (u - 2) // 2], 2048, ((u - 2) % 2) * UT
            xu[u] = xs[s0:s0 + UT]
            xh[u] = _halo(xs, s0)
            cr, vm = _unit_meta(S, s0)
            corr[u] = np.broadcast_to(cr.reshape(1, 128), (128, 128))
            vmask[u] = np.broadcast_to(vm.reshape(1, 32), (128, 32))
        m = dict(shared)
        m.update({
            "xu": xu, "xh": xh, "corr": corr, "vmask": vmask,
            "xop": np.ascontiguousarray(x_prompt[pi, (1 - hf) * 2048:(2 - hf) * 2048]),
            "mem": np.ascontiguousarray(np.concatenate([mem_prompt[pi:pi + 1], mem_sample[4 * c:4 * c + 4]], axis=0)),
            "tabP": cst["tabP"][hf],
        })
        in_maps.append(m)
    if "nc" not in _PROG:
        _PROG["nc"] = build_program()
    res = run_bass_kernel_spmd(_PROG["nc"], in_maps, core_ids=list(range(NCORES)))
    y_prompt = np.empty((4, 4096, D), np.float32)
    y_sample = np.empty((32, 2048, D), np.float32)
    for c in range(NCORES):
        yu = np.asarray(res.results[c]["yu"], dtype=np.float32)
        pi, hf = c // 2, c % 2
        for u in range(NUNIT):
            if u < 2:
                s0 = hf * 2048 + u * UT
                y_prompt[pi, s0:s0 + UT] = yu[u]
            else:
                s0 = ((u - 2) % 2) * UT
                y_sample[4 * c + (u - 2) // 2, s0:s0 + UT] = yu[u]
    return (y_prompt, y_sample)
```

```python
import contextlib
import numpy as np
import ml_dtypes
import concourse.bass as bass
import concourse.mybir as mybir
from concourse.bass_utils import run_bass_kernel_spmd

F32 = mybir.dt.float32
BF16 = mybir.dt.bfloat16
AF = mybir.ActivationFunctionType
ALU = mybir.AluOpType

NCORES = 8
D = 1024
KC = 8
UT = 1024
HAL = 16
XC = UT + 2 * HAL
NUNIT = 10
DFF = 2816
FC = 22
NMEM = 256
CW = 31
RMS_EPS = 1e-6
LN_EPS = 1e-5
TILES = ((0, 512), (512, 512), (1024, 32))

ENGS = ("pe", "act", "dve", "pool", "sp")
N_DMA_SEMS = 16
SAME_ENGINE_SYNC = True

V_NMIX, V_NXA, V_NMEM, V_NFFN = 0, 16, 32, 48
V_NFIN, V_PSC = 64, 72
V_CB, V_LG, V_LB = 80, 84, 88
V_CW = 92
NVEC = 92 + 4 * CW
XA_SCALE = 256 ** -0.5


class Op:
    __slots__ = ("eng", "fn", "reads", "writes", "dma", "idx", "sig", "count", "deps", "dsem", "dcount",
                 "prev_on_sem", "stage")

    def __init__(self, eng, fn, reads, writes, dma):
        self.eng, self.fn, self.reads, self.writes, self.dma = eng, fn, reads, writes, dma
        self.sig = False
        self.count = None
        self.deps = []
        self.dsem = None
        self.dcount = None
        self.prev_on_sem = None


class Prog:
    def __init__(self):
        self.ops = {e: [] for e in ENGS}
        self.last_w = {}
        self.readers = {}
        self.n = 0
        self.stage = "init"

    @staticmethod
    def _atoms(rs):
        out = []
        for r in rs:
            if isinstance(r, tuple) and len(r) == 3 and r[0] == "A":
                out.extend(("A", a) for a in range(r[1] // 256, (r[2] + 255) // 256))
            else:
                out.append(r)
        return out

    def op(self, eng, fn, reads=(), writes=(), dma=False):
        o = Op(eng, fn, self._atoms(reads), self._atoms(writes), dma)
        o.idx = self.n
        o.stage = self.stage
        self.n += 1
        deps = set()
        for r in o.reads:
            w = self.last_w.get(r)
            if w is not None:
                deps.add(w)
        for r in o.writes:
            w = self.last_w.get(r)
            if w is not None:
                deps.add(w)
            rd = self.readers.get(r)
            if rd:
                deps.update(rd.values())
        for r in o.reads:
            rd = self.readers.setdefault(r, {})
            rd[("dma", o.idx) if dma else eng] = o
        for r in o.writes:
            self.last_w[r] = o
            self.readers[r] = {}
        deps.discard(o)
        keep = []
        for d in deps:
            if d.eng == eng and not d.dma and not dma:
                if eng == "pe" or not SAME_ENGINE_SYNC:
                    continue
            keep.append(d)
        o.deps = keep
        for d in keep:
            d.sig = True
        self.ops[eng].append(o)
        return o

    def emit(self, nc, final_wait_ops=()):
        with contextlib.ExitStack() as st:
            esem = {e: st.enter_context(nc.semaphore("s_" + e)) for e in ENGS}
            dsems = {e: [st.enter_context(nc.semaphore("d_%s%d" % (e, i))) for i in range(N_DMA_SEMS)]
                     for e in ("sp", "act", "pool")}
            for e in ENGS:
                c = 0
                k = 0
                lastd = [None] * N_DMA_SEMS
                dcnt = [0] * N_DMA_SEMS
                for o in self.ops[e]:
                    if o.dma:
                        s = k % N_DMA_SEMS
                        k += 1
                        o.dsem = dsems[e][s]
                        dcnt[s] += 16
                        o.dcount = dcnt[s]
                        o.prev_on_sem = lastd[s]
                        lastd[s] = o
                    elif o.sig:
                        c += 1
                        o.count = c
            block = st.enter_context(nc.Block())

            def run_stream(e):
                def body(eng):
                    known = {}

                    def wait_for(d):
                        if d.dma:
                            key, sem, val = ("d", id(d.dsem)), d.dsem, d.dcount
                        else:
                            key, sem, val = ("e", d.eng), esem[d.eng], d.count
                        if known.get(key, 0) >= val:
                            return
                        eng.wait_ge(sem, val)
                        known[key] = val

                    for o in self.ops[e]:
                        for d in sorted(o.deps, key=lambda d: d.idx):
                            wait_for(d)
                        if o.dma and o.prev_on_sem is not None:
                            wait_for(o.prev_on_sem)
                        ins = o.fn(eng)
                        if o.dma:
                            ins.then_inc(o.dsem, 16)
                        elif o.sig:
                            ins.then_inc(esem[e], 1)
                    if e == "sp":
                        for d in final_wait_ops:
                            wait_for(d)
                return body

            block.tensor(run_stream("pe"))
            block.scalar(run_stream("act"))
            block.vector(run_stream("dve"))
            block.gpsimd(run_stream("pool"))
            block.sync(run_stream("sp"))


def I(m, *a, **k):
    return lambda e: getattr(e, m)(*a, **k)


class ABuf:
    def __init__(self, arena, off, n0, n1, dt):
        self.esz = 2 if dt == F32 else 1
        self.off, self.n0, self.n1 = off, n0, n1
        sl = arena[:, off:off + n0 * n1 * self.esz]
        if dt == F32:
            sl = sl.bitcast(F32)
        self.ap = sl.rearrange("p (a b) -> p a b", a=n0)
        self.end = off + n0 * n1 * self.esz

    def res(self, i, c0=None, c1=None):
        b = self.off + i * self.n1 * self.esz
        if c0 is None:
            return ("A", b, b + self.n1 * self.esz)
        return ("A", b + c0 * self.esz, b + c1 * self.esz)

    def allres(self):
        return [("A", self.off, self.end)]


_PROG = {}


def build_program():
    nc = bass.Bass("TRN2", target_bir_lowering=False)

    def din(name, shape, dt=F32):
        return nc.dram_tensor(name, list(shape), dt, kind="ExternalInput").ap()

    xu = din("xu", [NUNIT, UT, D])
    xop = din("xop", [2048, D])
    xh = din("xh", [NUNIT, 64, D])
    mem = din("mem", [5, NMEM, D])
    tabP = din("tabP", [2, 2, 2048, XC], BF16)
    tabS = din("tabS", [2, 2, 1024, XC], BF16)
    corr_d = din("corr", [NUNIT, 128, 128])
    vmask_d = din("vmask", [NUNIT, 128, 32])
    vecs_d = din("vecs", [128, NVEC])
    ident_d = din("ident", [128, 128])
    identb_d = din("identb", [128, 128], BF16)
    c128_d = din("c128", [128, 128], BF16)
    s128n_d = din("s128n", [128, 128], BF16)
    w_in = din("w_in_even", [1, D, 1536])
    w_out = din("w_out_even", [1, D, D])
    w_pool = din("w_pool", [1, 4, 256, 256])
    wq = din("xa_wq", [2, D, D])
    wkv = din("xa_wkv", [2, D, 2 * D])
    wo = din("xa_wo", [2, D, D])
    wgu = din("ffn_w_gate_up", [2, D, 2 * DFF])
    wdn = din("ffn_w_down", [2, DFF, D])
    yu = nc.dram_tensor("yu", [NUNIT, UT, D], F32, kind="ExternalOutput").ap()

    def dscratch(name, shape):
        return nc.dram_tensor(name, list(shape), BF16, kind="Internal").ap()

    b_in = dscratch("b_in", [D, 1536])
    b_out = dscratch("b_out", [D, D])
    b_pool = dscratch("b_pool", [1024, 256])
    b_q = dscratch("b_q", [2, D, D])
    b_kv = dscratch("b_kv", [2, D, 2 * D])
    b_o = dscratch("b_o", [2, D, D])
    b_gu = dscratch("b_gu", [2, D, 2 * DFF])
    b_dn = dscratch("b_dn", [2, DFF, D])
    b_diag = dscratch("b_diag", [4, CW * 128, 128])
    u_scr = dscratch("u_scr", [5, 32 * 128, 512])
    kv_scr = dscratch("kv_scr", [5, 2, 128, 4096])

    st = contextlib.ExitStack()
    with st:
        def T(name, shape, dt):
            return st.enter_context(nc.sbuf_tensor(name, list(shape), dt))

        xT = T("xT", [128, KC, XC], F32)
        xhT = T("xhT", [128, KC, 64], F32)
        hbuf = [T("hT0", [128, KC, 512], BF16), T("hT1", [128, KC, 512], BF16), T("hT2", [128, KC, 64], BF16)]
        RING_N = 4
        ring = [T("ring%d" % i, [128, 4096], BF16) for i in range(RING_N)]
        stg = [T("stg%d" % i, [128, D], F32) for i in range(3)]
        ostg = [T("ostg%d" % i, [128, D], F32) for i in range(2)]
        sq = T("sq", [128, KC, 512], BF16)
        rstd = [T("rstd%d" % i, [128, 512], F32) for i in range(2)]
        ident = T("identS", [128, 128], F32)
        identb = T("identbS", [128, 128], BF16)
        ones = T("onesS", [128, 128], BF16)
        ones512 = T("ones512S", [128, 128], BF16)
        ones1 = T("ones1S", [128, 128], BF16)
        c128 = T("c128S", [128, 128], BF16)
        s128n = T("s128nS", [128, 128], BF16)
        vecs = T("vecsS", [128, NVEC], F32)
        corr = T("corrS", [128, KC, 16], F32)
        vmask = T("vmaskS", [128, 32], F32)
        AREN = 43776
        arena = T("arena", [128, AREN], BF16)
        ps = [st.enter_context(nc.psum_tensor("ps%d" % i, [128, 512], F32)) for i in range(8)]

        P = Prog()
        state = {"bank": 0, "ring": 0, "stg": 0, "ostg": 0, "par": 0, "rs": 0, "ev": 0, "cst": 0}

        def bank():
            b = state["bank"]
            state["bank"] = (b + 1) % 8
            return b

        def evac_eng():
            state["ev"] ^= 1
            return "act" if state["ev"] else "dve"

        def copy_op(eng, out, in_, reads, writes):
            if eng == "act":
                return P.op("act", I("copy", out, in_), reads, writes)
            return P.op(eng, I("tensor_copy", out, in_), reads, writes)

        def wpiece(dram2d, kcn, ncols, key=None):
            s = state["ring"]
            state["ring"] = (s + 1) % RING_N
            view = ring[s][:, 0:kcn * ncols].rearrange("p (k n) -> p k n", k=kcn)
            src = dram2d.rearrange("(k p) n -> p k n", p=128)
            rk = [] if key is None else (list(key) if isinstance(key, list) else [key])
            P.op("sp", I("dma_start", out=view, in_=src), reads=rk, writes=[("ring", s)], dma=True)
            return view, ("ring", s)

        def wpiece_first(src2d, dst2d, kcn, ncols, key):
            i = state["cst"]
            state["cst"] ^= 1
            stgf = ABuf(arena, i * 8192, 1, 4096, F32)
            sview = stgf.ap[:, 0, 0:kcn * ncols].rearrange("p (k n) -> p k n", k=kcn)
            P.op("sp", I("dma_start", out=sview, in_=src2d.rearrange("(k p) n -> p k n", p=128)),
                 writes=stgf.allres(), dma=True)
            s = state["ring"]
            state["ring"] = (s + 1) % RING_N
            view = ring[s][:, 0:kcn * ncols].rearrange("p (k n) -> p k n", k=kcn)
            P.op("act", I("copy", view, sview), reads=stgf.allres(), writes=[("ring", s)])
            P.op("act", I("dma_start", out=dst2d.rearrange("(k p) n -> p k n", p=128), in_=view),
                 reads=[("ring", s)], writes=[key], dma=True)
            return view, ("ring", s)

        def mm(pb, n, lhsT, rhs, start, stop, reads):
            P.op("pe", I("matmul", ps[pb][:, 0:n], lhsT, rhs, start=start, stop=stop),
                 reads=reads, writes=[("ps", pb)])

        for dst, src, key in ((ident, ident_d, "ident"), (identb, identb_d, "identb"), (c128, c128_d, "c128"),
                              (s128n, s128n_d, "s128n"), (vecs, vecs_d, "vecs")):
            P.op("sp", I("dma_start", out=dst[:], in_=src), writes=[key], dma=True)
        P.op("dve", I("memset", ones[:], 1.0 / 1024), writes=["ones"])
        P.op("dve", I("memset", ones512[:], 1.0 / 512), writes=["ones512"])
        P.op("dve", I("memset", ones1[:], 1.0), writes=["ones1"])

        dtmp = ABuf(arena, 0, CW, 128, BF16)
        for c in range(4):
            for t in range(CW):
                col = V_CW + c * CW + t
                P.op("dve", I("tensor_scalar_mul", dtmp.ap[:, t, :], identb[:], vecs[:, col:col + 1]),
                     reads=["identb", "vecs"], writes=dtmp.allres())
            P.op("sp", I("dma_start", out=b_diag[c].rearrange("(t p) n -> p t n", p=128), in_=dtmp.ap),
                 reads=dtmp.allres(), writes=[("b_diag", c)], dma=True)
        def cast(dst, src, key, nsplit=1):
            d2 = dst.rearrange("(p a) n -> p (a n)", p=128)
            s2 = src.rearrange("(p a) n -> p (a n)", p=128)
            step = d2.shape[1] // nsplit
            for i in range(nsplit):
                P.op("pool", I("dma_start", out=d2[:, i * step:(i + 1) * step],
                                                       in_=s2[:, i * step:(i + 1) * step]),
                     reads=[("b_diag", c) for c in range(4)], writes=[(key, i)], dma=True)

        cast(b_in, w_in[0], "b_in")
        cast(b_out, w_out[0], "b_out")
        for l in range(2):
            cast(b_kv[l], wkv[l], ("b_kv", l))
            cast(b_q[l], wq[l], ("b_q", l))
            cast(b_o[l], wo[l], ("b_o", l))
            if l == 0:
                cast(b_pool, w_pool[0].rearrange("g k n -> (g k) n"), "b_pool")
        fence_reads = [("b_in", 0), ("b_out", 0), ("b_pool", 0)]
        for l in range(2):
            fence_reads += [(("b_q", l), 0), (("b_kv", l), 0), (("b_o", l), 0)]
        fence_reads += [("b_diag", c) for c in range(4)]

        P.op("sp", I("dma_start", out=vmask[:], in_=vmask_d[0]), reads=fence_reads, writes=["vmask"], dma=True)

        def load_T(src_rows, nt, dst_ap_fn, dst_res_fn, ev=None):
            s = state["stg"]
            state["stg"] = (s + 1) % 3
            P.op("sp", I("dma_start", out=stg[s][0:nt, :], in_=src_rows), writes=[("stg", s)], dma=True)
            for b in range(2):
                pb = bank()
                for j in range(4):
                    dc = b * 4 + j
                    P.op("pe", I("transpose",
                        ps[pb][:, j * 128:j * 128 + nt], stg[s][0:nt, dc * 128:(dc + 1) * 128], ident[0:nt, 0:nt]),
                        reads=[("stg", s), "ident"], writes=[("ps", pb)])
                src = ps[pb][:].rearrange("p (j n) -> p j n", j=4)[:, :, 0:nt]
                copy_op(ev or evac_eng(), dst_ap_fn(b * 4), src, [("ps", pb)], [dst_res_fn(b * 4 + j) for j in range(4)])

        def rmsnorm(x_ap, x_res, n, gcol, out_ap, out_res, mask_ap=None, mask_res=None):
            r = state["rs"]
            state["rs"] ^= 1
            for hf in range(2):
                P.op("act", I("activation", sq[:, hf * 4:(hf + 1) * 4, 0:n], x_ap[:, hf * 4:(hf + 1) * 4, :], AF.Square),
                     reads=x_res[hf * 4:(hf + 1) * 4], writes=[("sq", hf)])
            pb = bank()
            for dc in range(KC):
                mm(pb, n, ones[:], sq[:, dc, 0:n], dc == 0, dc == KC - 1, [("sq", dc // 4), "ones"])
            P.op("act", I("activation", rstd[r][:, 0:n], ps[pb][:, 0:n], AF.Ln, bias=RMS_EPS),
                 reads=[("ps", pb)], writes=[("rstd", r)])
            P.op("act", I("activation", rstd[r][:, 0:n], rstd[r][:, 0:n], AF.Exp, scale=-0.5),
                 reads=[("rstd", r)], writes=[("rstd", r)])
            if mask_ap is not None:
                P.op("dve", I("tensor_mul", rstd[r][:, 0:n], rstd[r][:, 0:n], mask_ap),
                     reads=[("rstd", r), mask_res], writes=[("rstd", r)])
            for dc in range(KC):
                P.op("dve", I("scalar_tensor_tensor",
                    out_ap[:, dc, :], x_ap[:, dc, :], vecs[:, gcol + dc:gcol + dc + 1], rstd[r][:, 0:n],
                    ALU.mult, ALU.mult),
                    reads=[x_res[dc], ("rstd", r), "vecs"], writes=[out_res[dc]])

        def xres(ti):
            return [("xT", dc, ti) for dc in range(KC)]

        def hres(hi):
            return [("hT", hi, dc) for dc in range(KC)]

        def norm_tile(ti, gcol, mask_ap=None, mask_res=None):
            c0, n = TILES[ti]
            rmsnorm(xT[:, :, c0:c0 + n], xres(ti), n, gcol, hbuf[ti][:, :, 0:n], hres(ti), mask_ap, mask_res)

        def resid_add(ti, mc, pb, scale_col=None):
            c0, n = TILES[ti]
            dst = xT[:, mc, c0:c0 + n]
            if scale_col is None:
                P.op("dve", I("tensor_add", dst, dst, ps[pb][:, 0:n]),
                     reads=[("ps", pb), ("xT", mc, ti)], writes=[("xT", mc, ti)])
            else:
                P.op("dve", I("scalar_tensor_tensor", dst, ps[pb][:, 0:n], vecs[:, scale_col:scale_col + 1],
                                                            dst, ALU.mult, ALU.add),
                     reads=[("ps", pb), ("xT", mc, ti), "vecs"], writes=[("xT", mc, ti)])

        def proj_resid(wdram, key, rhs_fn, tiles):
            for ti in tiles:
                n = TILES[ti][1]
                for half in range(2):
                    wv, wr = wpiece(wdram[:, half * 512:(half + 1) * 512], KC, 512, key)
                    for m4 in range(4):
                        pb = bank()
                        for kc in range(KC):
                            ra, rr = rhs_fn(ti, kc)
                            mm(pb, n, wv[:, kc, m4 * 128:(m4 + 1) * 128], ra, kc == 0, kc == KC - 1, [wr, rr])
                        resid_add(ti, half * 4 + m4, pb)

        def attn_bufs():
            return dict(q=ABuf(arena, 0, 8, 512, BF16), o=ABuf(arena, 4096, 8, 512, BF16),
                        Pt=ABuf(arena, 8192, 4, 512, BF16), KT=ABuf(arena, 10240, 8, 256, BF16),
                        V=ABuf(arena, 12288, 2, 1024, BF16), rden=ABuf(arena, 14336, 2, 512, F32),
                        mn=ABuf(arena, 18432, 8, 256, BF16), memT=ABuf(arena, 20480, 8, 256, F32))

        def attn_kv_load(seq):
            memT = attn_bufs()["memT"]
            for j in range(2):
                load_T(mem[seq][j * 128:(j + 1) * 128], 128,
                       lambda dc0, j=j: memT.ap[:, dc0:dc0 + 4, j * 128:(j + 1) * 128],
                       lambda dc: memT.res(dc), ev="act")

        def attn_kv(l, seq, is_a):
            B = attn_bufs()
            memT, mn, KT, V = B["memT"], B["mn"], B["KT"], B["V"]
            kvap = arena[:, KT.off:V.end]
            kvres = KT.allres() + V.allres()
            if not is_a:
                P.op("sp", I("dma_start", out=kvap, in_=kv_scr[seq][l]), reads=[("kv_scr", seq, l)], writes=kvres, dma=True)
                return
            attn_kv_load(seq)
            rmsnorm(memT.ap[:, :, :], [memT.res(dc) for dc in range(KC)], 256, V_NMEM + l * 8,
                    mn.ap[:, :, :], [mn.res(dc) for dc in range(KC)])
            for half in range(2):
                wv, wr = wpiece(b_kv[l][:, half * 512:(half + 1) * 512], KC, 512, (("b_kv", l), 0))
                for m4 in range(4):
                    oc = half * 4 + m4
                    pb = bank()
                    for kc in range(KC):
                        mm(pb, 256, wv[:, kc, m4 * 128:(m4 + 1) * 128], mn.ap[:, kc, :], kc == 0, kc == KC - 1,
                           [wr, mn.res(kc)])
                    copy_op("act", KT.ap[:, oc, :], ps[pb][:, 0:256], [("ps", pb)], [KT.res(oc)])
            for half in range(2):
                wv, wr = wpiece(b_kv[l][:, 1024 + half * 512:1024 + (half + 1) * 512], KC, 512, (("b_kv", l), 0))
                for j in range(2):
                    pb = bank()
                    for kc in range(KC):
                        mm(pb, 512, mn.ap[:, kc, j * 128:(j + 1) * 128], wv[:, kc, :], kc == 0, kc == KC - 1,
                           [wr, mn.res(kc)])
                    copy_op("act", V.ap[:, j, half * 512:(half + 1) * 512], ps[pb][:, :], [("ps", pb)],
                            [V.res(j, half * 512, (half + 1) * 512)])
            P.op("sp", I("dma_start", out=kv_scr[seq][l], in_=kvap), reads=kvres, writes=[("kv_scr", seq, l)], dma=True)

        def attention(l, seq, tiles, is_a, kv_done=False):
            B = attn_bufs()
            q, o, Pt, KT, V, rden = B["q"], B["o"], B["Pt"], B["KT"], B["V"], B["rden"]
            pti = [0]

            def qproj(ti):
                n = TILES[ti][1]
                for half in range(2):
                    wv, wr = wpiece(b_q[l][:, half * 512:(half + 1) * 512], KC, 512, (("b_q", l), 0))
                    for m4 in range(4):
                        mc = half * 4 + m4
                        pb = bank()
                        for kc in range(KC):
                            mm(pb, n, wv[:, kc, m4 * 128:(m4 + 1) * 128], hbuf[ti][:, kc, 0:n], kc == 0, kc == KC - 1,
                               [wr, ("hT", ti, kc)])
                        copy_op(evac_eng(), q.ap[:, mc, 0:n], ps[pb][:, 0:n], [("ps", pb)], [q.res(mc)])

            def heads(ti):
                n = TILES[ti][1]

                def scores(h, pbuf):
                    for j in range(2):
                        pb = bank()
                        for dd in range(2):
                            dc = 2 * h + dd
                            mm(pb, n, KT.ap[:, dc, j * 128:(j + 1) * 128], q.ap[:, dc, 0:n], dd == 0, dd == 1,
                               [KT.res(dc), q.res(dc)])
                        P.op("act", I("activation", Pt.ap[:, pbuf * 2 + j, 0:n], ps[pb][:, 0:n], AF.Exp,
                                      scale=float(XA_SCALE)),
                             reads=[("ps", pb)], writes=[Pt.res(pbuf * 2 + j)])

                def rest(h, pbuf):
                    pb = bank()
                    for j in range(2):
                        mm(pb, n, ones1[:], Pt.ap[:, pbuf * 2 + j, 0:n], j == 0, j == 1, ["ones1", Pt.res(pbuf * 2 + j)])
                    rd = rden.ap[:, pbuf, 0:n]
                    P.op("act", I("activation", rd, ps[pb][:, 0:n], AF.Ln), reads=[("ps", pb)], writes=[rden.res(pbuf)])
                    P.op("act", I("activation", rd, rd, AF.Exp, scale=-1.0), reads=[rden.res(pbuf)], writes=[rden.res(pbuf)])
                    for dd in range(2):
                        dc = 2 * h + dd
                        pb = bank()
                        for j in range(2):
                            mm(pb, n, V.ap[:, j, dc * 128:(dc + 1) * 128], Pt.ap[:, pbuf * 2 + j, 0:n], j == 0, j == 1,
                               [V.res(j), Pt.res(pbuf * 2 + j)])
                        P.op("dve", I("tensor_mul", o.ap[:, dc, 0:n], ps[pb][:, 0:n], rd),
                             reads=[("ps", pb), rden.res(pbuf)], writes=[o.res(dc)])

                pbs_ = [(pti[0] + h) % 2 for h in range(4)]
                pti[0] += 4
                scores(0, pbs_[0])
                for h in range(4):
                    if h + 1 < 4:
                        scores(h + 1, pbs_[h + 1])
                    rest(h, pbs_[h])

            def oproj(ti):
                proj_resid(b_o[l], (("b_o", l), 0), lambda ti_, kc: (o.ap[:, kc, 0:TILES[ti_][1]], o.res(kc)), [ti])

            norm_tile(tiles[0], V_NXA + l * 8)
            qproj(tiles[0])
            if not kv_done:
                attn_kv(l, seq, is_a)
            for idx, ti in enumerate(tiles):
                nxt = tiles[idx + 1] if idx + 1 < len(tiles) else None
                heads(ti)
                if nxt is not None:
                    norm_tile(nxt, V_NXA + l * 8)
                oproj(ti)
                if nxt is not None:
                    qproj(nxt)

        def ffn(l, tiles, first=False):
            sg = ABuf(arena, 16384, 2, 512, F32)
            act = ABuf(arena, 20480, FC, XC, BF16)
            assert act.end <= AREN
            for ti in tiles:
                norm_tile(ti, V_NFFN + l * 8)
            sgi = 0
            for pc in range(6):
                ncol = 512 if pc < 5 else 256
                if first:
                    gv, gr = wpiece_first(wgu[l][:, pc * 512:pc * 512 + ncol], b_gu[l][:, pc * 512:pc * 512 + ncol],
                                          KC, ncol, ("b_gu", l, pc, 0))
                    uv, ur = wpiece_first(wgu[l][:, DFF + pc * 512:DFF + pc * 512 + ncol],
                                          b_gu[l][:, DFF + pc * 512:DFF + pc * 512 + ncol], KC, ncol, ("b_gu", l, pc, 1))
                else:
                    gv, gr = wpiece(b_gu[l][:, pc * 512:pc * 512 + ncol], KC, ncol, ("b_gu", l, pc, 0))
                    uv, ur = wpiece(b_gu[l][:, DFF + pc * 512:DFF + pc * 512 + ncol], KC, ncol, ("b_gu", l, pc, 1))
                for ti in tiles:
                    c0, n = TILES[ti]
                    for jj in range(ncol // 128):
                        ch = pc * 4 + jj
                        pg, pu = bank(), bank()
                        for kc in range(KC):
                            mm(pg, n, gv[:, kc, jj * 128:(jj + 1) * 128], hbuf[ti][:, kc, 0:n], kc == 0, kc == KC - 1,
                               [gr, ("hT", ti, kc)])
                        for kc in range(KC):
                            mm(pu, n, uv[:, kc, jj * 128:(jj + 1) * 128], hbuf[ti][:, kc, 0:n], kc == 0, kc == KC - 1,
                               [ur, ("hT", ti, kc)])
                        sb = sgi % 2
                        sgi += 1
                        sgd = sg.ap[:, sb, 0:n]
                        P.op("act", I("activation", sgd, ps[pg][:, 0:n], AF.Silu),
                             reads=[("ps", pg)], writes=[sg.res(sb)])
                        ad = act.ap[:, ch, c0:c0 + n]
                        P.op("dve", I("tensor_mul", ad, ps[pu][:, 0:n], sgd),
                             reads=[("ps", pu), sg.res(sb)], writes=[act.res(ch, c0, c0 + n)])
            for m2 in range(4):
                if first:
                    wa, war = wpiece_first(wdn[l][0:1408, m2 * 256:(m2 + 1) * 256], b_dn[l][0:1408, m2 * 256:(m2 + 1) * 256],
                                           11, 256, ("b_dn", l, m2, 0))
                    wb, wbr = wpiece_first(wdn[l][1408:2816, m2 * 256:(m2 + 1) * 256],
                                           b_dn[l][1408:2816, m2 * 256:(m2 + 1) * 256], 11, 256, ("b_dn", l, m2, 1))
                else:
                    wa, war = wpiece(b_dn[l][0:1408, m2 * 256:(m2 + 1) * 256], 11, 256, ("b_dn", l, m2, 0))
                    wb, wbr = wpiece(b_dn[l][1408:2816, m2 * 256:(m2 + 1) * 256], 11, 256, ("b_dn", l, m2, 1))
                for ti in tiles:
                    c0, n = TILES[ti]
                    for mm_ in range(2):
                        pb = bank()
                        for kc in range(FC):
                            wv, wr = (wa, war) if kc < 11 else (wb, wbr)
                            mm(pb, n, wv[:, kc % 11, mm_ * 128:(mm_ + 1) * 128], act.ap[:, kc, c0:c0 + n],
                               kc == 0, kc == FC - 1, [wr, act.res(kc, c0, c0 + n)])
                        resid_add(ti, m2 * 2 + mm_, pb)

        out_ops = []
        for u in range(NUNIT):
            is_p = u < 2
            nsc = 32 if is_p else 16
            seq = 0 if is_p else 1 + (u - 2) // 2
            tab = tabP[u] if is_p else tabS[(u - 2) % 2]
            others = []
            if u % 2 == 0:
                others = [xu[u + 1][i * 512:(i + 1) * 512] for i in range(2)]
                if is_p:
                    others += [xop[i * 512:(i + 1) * 512] for i in range(4)]

            P.op("sp", I("dma_start", out=corr[:].rearrange("p a b -> p (a b)"), in_=corr_d[u]),
                 writes=["corr"], dma=True)
            P.op("sp", I("dma_start", out=vmask[:], in_=vmask_d[u]), writes=["vmask"], dma=True)

            Utm = ABuf(arena, 0, nsc, 512, BF16)
            hglu = ABuf(arena, 16384, 4, 1088, BF16)
            hc = ABuf(arena, hglu.end, 4, 512, F32)
            hcb = ABuf(arena, hc.end, 4, 512, BF16)
            hsq = ABuf(arena, hcb.end, 4, 512, BF16)
            xo = ABuf(arena, hc.off, 8, 512, F32)
            assert xo.end == hsq.end
            cat = ABuf(arena, hsq.end, 8, 512, BF16)
            ABt = [ABuf(arena, cat.end, 8, 512, BF16), ABuf(arena, cat.end + 4096, 8, 512, BF16)]
            lnm = ABuf(arena, ABt[0].off, 3, 512, F32)
            sgt = ABuf(arena, ABt[0].off, 1, 512, F32)
            adt = ABuf(arena, ABt[1].off, 1, 512, BF16)
            assert ABt[1].end <= AREN, ABt[1].end

            P.stage = "load"
            for tcn in range(8):
                ti = tcn // 4
                load_T(xu[u][tcn * 128:(tcn + 1) * 128], 128,
                       lambda dc0, tcn=tcn: xT[:, dc0:dc0 + 4, tcn * 128:(tcn + 1) * 128],
                       lambda dc, ti=ti: ("xT", dc, ti))
            load_T(xh[u], 64, lambda dc0: xhT[:, dc0:dc0 + 4, :], lambda dc: ("xhT", dc))
            for dc0 in (0, 4):
                copy_op("dve", xT[:, dc0:dc0 + 4, UT:XC], xhT[:, dc0:dc0 + 4, 16:48],
                        [("xhT", dc0 + j) for j in range(4)], [("xT", dc0 + j, 2) for j in range(4)])

            P.stage = "inproj"
            def glu(valw, gatew, hb, hi, n, dsts):
                for c in range(4):
                    pv, pg = bank(), bank()
                    for (pb, (wv, wr)) in ((pv, valw), (pg, gatew)):
                        for kc in range(KC):
                            mm(pb, n, wv[:, kc, c * 128:(c + 1) * 128], hb[:, kc, 0:n], kc == 0, kc == KC - 1,
                               [wr, ("hT", hi, kc)])
                    P.op("act", I("activation", sgt.ap[:, 0, 0:n], ps[pg][:, 0:n], AF.Sigmoid),
                         reads=[("ps", pg)], writes=sgt.allres())
                    for (d0, s0, w_) in dsts:
                        dstg = hglu.ap[:, c, d0:d0 + w_]
                        P.op("dve", I("tensor_mul",
                            dstg, ps[pv][:, s0:s0 + w_], sgt.ap[:, 0, s0:s0 + w_]),
                            reads=[("ps", pv)] + sgt.allres(), writes=[hglu.res(c, d0, d0 + w_)])

            def uproj(fw_, hb, hi, sc0):
                wv, wr = fw_
                for tcn in range(4):
                    pb = bank()
                    for kc in range(KC):
                        mm(pb, 512, hb[:, kc, tcn * 128:(tcn + 1) * 128], wv[:, kc, :], kc == 0, kc == KC - 1,
                           [wr, ("hT", hi, kc)])
                    copy_op(evac_eng(), Utm.ap[:, sc0 + tcn, :], ps[pb][:, :], [("ps", pb)], [Utm.res(sc0 + tcn)])

            is_a = (u % 2 == 0)
            useq = u_scr[seq][0:nsc * 128].rearrange("(k p) n -> p k n", p=128)
            xo2 = ABuf(arena, hsq.end, 8, 512, F32)
            norm_tile(0, V_NMIX + 0)
            norm_tile(1, V_NMIX + 0)
            P.stage = "inproj_halo"
            rmsnorm(xhT[:, :, :], [("xhT", dc) for dc in range(KC)], 64, V_NMIX + 0, hbuf[2][:, :, 0:64], hres(2))
            P.stage = "inproj"
            if not is_a:
                P.op("sp", I("dma_start", out=Utm.ap, in_=useq), reads=[("u_scr", seq)], writes=Utm.allres(), dma=True)
            for ti in range(2):
                valw = wpiece(b_in[:, 0:512], KC, 512, ("b_in", 0))
                gatew = wpiece(b_in[:, 512:1024], KC, 512, ("b_in", 0))
                glu(valw, gatew, hbuf[ti], ti, 512, [(32 + ti * 512, 0, 512)])
                if is_a:
                    fw_ = wpiece(b_in[:, 1024:1536], KC, 512, ("b_in", 0))
                    uproj(fw_, hbuf[ti], ti, ti * 4)
            P.stage = "inproj_halo"
            valw = wpiece(b_in[:, 0:512], KC, 512, ("b_in", 0))
            gatew = wpiece(b_in[:, 512:1024], KC, 512, ("b_in", 0))
            glu(valw, gatew, hbuf[2], 2, 64, [(0, 0, 32), (32 + UT, 32, 32)])
            P.stage = "others"
            if is_a:
                xobufs = [xo, xo2]

                def load_other(oi):
                    xb = xobufs[oi % 2]
                    for tcn in range(4):
                        load_T(others[oi][tcn * 128:(tcn + 1) * 128], 128,
                               lambda dc0, tcn=tcn, xb=xb: xb.ap[:, dc0:dc0 + 4, tcn * 128:(tcn + 1) * 128],
                               lambda dc, xb=xb: xb.res(dc))

                def norm_other(oi):
                    xb = xobufs[oi % 2]
                    rmsnorm(xb.ap[:, :, :], [xb.res(dc) for dc in range(KC)], 512, V_NMIX + 0, hbuf[oi % 2][:, :, :],
                            hres(oi % 2))

                load_other(0)
                norm_other(0)
                for oi in range(len(others)):
                    if oi + 1 < len(others):
                        load_other(oi + 1)
                    fw_ = wpiece(b_in[:, 1024:1536], KC, 512, ("b_in", 0))
                    if oi + 1 < len(others):
                        norm_other(oi + 1)
                    uproj(fw_, hbuf[oi % 2], oi % 2, 8 + oi * 4)
                    hh = nsc // 2
                    for hi_c in range(8 + oi * 4, 12 + oi * 4):
                        sc = hi_c - hh
                        if sc < 0:
                            continue
                        P.op("dve", I("tensor_add", adt.ap[:, 0, :], Utm.ap[:, sc, :], Utm.ap[:, sc + hh, :]),
                             reads=[Utm.res(sc), Utm.res(sc + hh)], writes=adt.allres())
                        P.op("dve", I("tensor_sub", Utm.ap[:, sc + hh, :], Utm.ap[:, sc, :], Utm.ap[:, sc + hh, :]),
                             reads=[Utm.res(sc), Utm.res(sc + hh)], writes=[Utm.res(sc + hh)])
                        P.op("dve", I("tensor_copy", Utm.ap[:, sc, :], adt.ap[:, 0, :]),
                             reads=adt.allres(), writes=[Utm.res(sc)])
                P.op("sp", I("dma_start", out=useq, in_=Utm.ap), reads=Utm.allres(), writes=[("u_scr", seq)], dma=True)

            for ti in range(3):
                c0, n = TILES[ti]
                P.stage = "conv%d" % ti
                if ti < 2:
                    segs = [(0, c0 + 17, n)]
                else:
                    segs = [(0, 1, 16), (16, 1024 + 17, 16)]
                for c in range(4):
                    dg, dgr = wpiece(b_diag[c], CW, 128, ("b_diag", c))
                    pb = bank()
                    for (p0, h0, w_) in segs:
                        for t in range(CW):
                            P.op("pe", I("matmul",
                                ps[pb][:, p0:p0 + w_], dg[:, t, :], hglu.ap[:, c, h0 + t:h0 + t + w_],
                                start=(t == 0), stop=(t == CW - 1)),
                                reads=[dgr, hglu.res(c)], writes=[("ps", pb)])
                    P.op("act", I("activation", hc.ap[:, c, 0:n], ps[pb][:, 0:n], AF.Identity,
                                                                  bias=vecs[:, V_CB + c:V_CB + c + 1]),
                         reads=[("ps", pb), "vecs"], writes=[hc.res(c)])
                    P.op("act", I("activation", hsq.ap[:, c, 0:n], ps[pb][:, 0:n], AF.Square,
                                                                  bias=vecs[:, V_CB + c:V_CB + c + 1]),
                         reads=[("ps", pb), "vecs"], writes=[hsq.res(c)])
                    copy_op("dve", hcb.ap[:, c, 0:n], hc.ap[:, c, 0:n], [hc.res(c)], [hcb.res(c)])
                pm, pq = bank(), bank()
                for c in range(4):
                    mm(pm, n, ones512[:], hcb.ap[:, c, 0:n], c == 0, c == 3, ["ones512", hcb.res(c)])
                for c in range(4):
                    mm(pq, n, ones512[:], hsq.ap[:, c, 0:n], c == 0, c == 3, ["ones512", hsq.res(c)])
                mean, rs_, tmp = lnm.ap[:, 0, 0:n], lnm.ap[:, 1, 0:n], lnm.ap[:, 2, 0:n]
                copy_op("dve", mean, ps[pm][:, 0:n], [("ps", pm)], [lnm.res(0)])
                P.op("dve", I("tensor_mul", tmp, mean, mean), reads=[lnm.res(0)], writes=[lnm.res(2)])
                P.op("dve", I("tensor_sub", rs_, ps[pq][:, 0:n], tmp),
                     reads=[("ps", pq), lnm.res(2)], writes=[lnm.res(1)])
                P.op("act", I("activation", rs_, rs_, AF.Ln, bias=LN_EPS),
                     reads=[lnm.res(1)], writes=[lnm.res(1)])
                P.op("act", I("activation", rs_, rs_, AF.Exp, scale=-0.5), reads=[lnm.res(1)], writes=[lnm.res(1)])
                for c in range(4):
                    hcc = hc.ap[:, c, 0:n]
                    P.op("dve", I("tensor_sub", hcc, hcc, mean),
                         reads=[hc.res(c), lnm.res(0)], writes=[hc.res(c)])
                    P.op("dve", I("tensor_mul", hcc, hcc, rs_),
                         reads=[hc.res(c), lnm.res(1)], writes=[hc.res(c)])
                    P.op("act", I("activation",
                        cat.ap[:, c, 0:n], hcc, AF.Silu, bias=vecs[:, V_LB + c:V_LB + c + 1],
                        scale=vecs[:, V_LG + c:V_LG + c + 1]),
                        reads=[hc.res(c), "vecs"], writes=[cat.res(c)])
                P.stage = "dft%d" % ti
                if ti != 1:
                    hh = nsc // 2
                    tcol, gw = (0, 512) if ti == 0 else (1024, 16)
                    for cs in range(2):
                        for par in range(2):
                            pbs = [bank() for _ in range(4)]
                            for pi in range(hh // 8):
                                if ti == 0:
                                    tv, tr = wpiece(tab[cs][pi * 1024:(pi + 1) * 1024, par * 512:(par + 1) * 512], 8, 512)
                                    tsl = lambda s8: tv[:, s8, :]
                                else:
                                    tv, tr = wpiece(tab[cs][pi * 1024:(pi + 1) * 1024, 1024:1056], 8, 32)
                                    tsl = lambda s8, tv=tv, par=par: tv[:, s8, par * 16:(par + 1) * 16]
                                for s8 in range(8):
                                    sc = pi * 8 + s8
                                    src = sc + par * hh
                                    for c in range(4):
                                        mm(pbs[c], gw, Utm.ap[:, src, c * 128:(c + 1) * 128], tsl(s8), sc == 0, sc == hh - 1,
                                           [tr, Utm.res(src)])
                            for c in range(4):
                                if ti == 0:
                                    dst2 = arena[:, ABt[0].off:ABt[0].off + 8192].rearrange(
                                        "p (t r c) -> p t r c", t=2, r=8)[:, :, cs * 4 + c, par:512:2]
                                    src2 = ps[pbs[c]][:, :].rearrange("p (t c) -> p t c", t=2)
                                    copy_op(evac_eng(), dst2, src2, [("ps", pbs[c])],
                                            [ABt[0].res(cs * 4 + c), ABt[1].res(cs * 4 + c)])
                                else:
                                    copy_op(evac_eng(), ABt[0].ap[:, cs * 4 + c, par:32:2], ps[pbs[c]][:, 0:16],
                                            [("ps", pbs[c])], [ABt[0].res(cs * 4 + c)])
                AB = ABt[ti % 2]
                for g in range(4):
                    pb = bank()
                    mm(pb, n, c128[:], AB.ap[:, g, 0:n], True, False, ["c128", AB.res(g)])
                    mm(pb, n, s128n[:], AB.ap[:, 4 + g, 0:n], False, True, ["s128n", AB.res(4 + g)])
                    copy_op(evac_eng(), cat.ap[:, 4 + g, 0:n], ps[pb][:, 0:n], [("ps", pb)], [cat.res(4 + g)])
                proj_resid(b_out, ("b_out", 0), lambda ti_, kc: (cat.ap[:, kc, 0:TILES[ti_][1]], cat.res(kc)), [ti])

            P.stage = "attn0"
            attention(0, seq, [0, 1, 2], is_a)
            P.stage = "ffn0"
            ffn(0, [0, 1, 2], first=(u == 0))

            P.stage = "pool"
            hpad = ABuf(arena, 0, 8, XC, BF16)
            for ti in range(2):
                c0, n = TILES[ti]
                rmsnorm(xT[:, :, c0:c0 + n], xres(ti), n, V_NMIX + 8, hpad.ap[:, :, 16 + c0:16 + c0 + n],
                        [hpad.res(dc, 16 + c0, 16 + c0 + n) for dc in range(KC)])
            norm_tile(2, V_NMIX + 8, vmask[:, :], "vmask")
            for dc0 in (0, 4):
                copy_op("dve", hpad.ap[:, dc0:dc0 + 4, 0:16], hbuf[2][:, dc0:dc0 + 4, 0:16],
                        [("hT", 2, dc0 + j) for j in range(4)], [hpad.res(dc0 + j, 0, 16) for j in range(4)])
                copy_op("dve", hpad.ap[:, dc0:dc0 + 4, 16 + UT:XC], hbuf[2][:, dc0:dc0 + 4, 16:32],
                        [("hT", 2, dc0 + j) for j in range(4)], [hpad.res(dc0 + j, 16 + UT, XC) for j in range(4)])
            attn_kv(1, seq, is_a)
            pbf = ABuf(arena, 24576, 8, UT, BF16)
            te = ABuf(arena, pbf.end, 2, 8, F32)
            tei = 0
            for ti in range(2):
                c0, n = TILES[ti]
                for dc in range(KC):
                    w_ = 2 << (dc // 2)
                    base = 16 + c0 - w_ // 2
                    pb = bank()
                    for j in range(w_):
                        mm(pb, n, identb[:], hpad.ap[:, dc, base + j:base + j + n], j == 0, j == w_ - 1,
                           ["identb", hpad.res(dc)])
                    P.op("dve", I("scalar_tensor_tensor", pbf.ap[:, dc, c0:c0 + n], ps[pb][:, 0:n], 1.0 / w_,
                                  hpad.ap[:, dc, 16 + c0:16 + c0 + n], ALU.mult, ALU.subtract),
                         reads=[("ps", pb), hpad.res(dc)], writes=[pbf.res(dc, c0, c0 + n)])
                    e0, k0 = (0, 0) if ti == 0 else (n - 8, 8)
                    tea = te.ap[:, tei % 2, :]
                    ter = te.res(tei % 2)
                    tei += 1
                    P.op("dve", I("tensor_mul", tea, ps[pb][:, e0:e0 + 8], corr[:, dc, k0:k0 + 8]),
                         reads=[("ps", pb), "corr"], writes=[ter])
                    P.op("dve", I("tensor_sub", pbf.ap[:, dc, c0 + e0:c0 + e0 + 8], tea,
                                  hpad.ap[:, dc, 16 + c0 + e0:16 + c0 + e0 + 8]),
                         reads=[ter, hpad.res(dc), pbf.res(dc, c0, c0 + n)], writes=[pbf.res(dc, c0, c0 + n)])
            pw, pwr = wpiece(b_pool, 8, 256, ("b_pool", 0))
            for ti in range(2):
                c0, n = TILES[ti]
                for g in range(4):
                    for oc in range(2):
                        pb = bank()
                        for kc in range(2):
                            mm(pb, n, pw[:, g * 2 + kc, oc * 128:(oc + 1) * 128], pbf.ap[:, g * 2 + kc, c0:c0 + n],
                               kc == 0, kc == 1, [pwr, pbf.res(g * 2 + kc, c0, c0 + n)])
                        resid_add(ti, g * 2 + oc, pb, V_PSC + g * 2 + oc)

            P.stage = "attn1"
            attention(1, seq, [0, 1], is_a, kv_done=True)
            P.stage = "ffn1"
            ffn(1, [0, 1], first=(u == 0))

            P.stage = "final"
            yT = ABuf(arena, 0, 8, UT, F32)
            for ti in range(2):
                c0, n = TILES[ti]
                rmsnorm(xT[:, :, c0:c0 + n], xres(ti), n, V_NFIN, yT.ap[:, :, c0:c0 + n],
                        [yT.res(dc, c0, c0 + n) for dc in range(KC)])
            for tcn in range(8):
                ti = tcn // 4
                s = state["ostg"]
                state["ostg"] = (s + 1) % 2
                for b in range(2):
                    pb = bank()
                    for j in range(4):
                        dc = b * 4 + j
                        P.op("pe", I("transpose",
                            ps[pb][:, j * 128:(j + 1) * 128], yT.ap[:, dc, tcn * 128:(tcn + 1) * 128], ident[:]),
                            reads=[yT.res(dc, tcn * 128, (tcn + 1) * 128), "ident"], writes=[("ps", pb)])
                    copy_op(evac_eng(), ostg[s][:, b * 512:(b + 1) * 512], ps[pb][:, :], [("ps", pb)], [("ostg", s, b)])
                out_ops.append(P.op("act", I("dma_start",
                    out=yu[u][tcn * 128:(tcn + 1) * 128, :], in_=ostg[s][:, :]), reads=[("ostg", s, 0), ("ostg", s, 1)], dma=True))

        P.emit(nc, final_wait_ops=out_ops)
        _PROG["P"] = P
    return nc


_BF = ml_dtypes.bfloat16
_CACHE = {}


def _tables(S, s0, others_pos):
    rows = np.concatenate(others_pos).astype(np.int64)[:S // 2]
    own = np.arange(s0, s0 + UT)
    halo = np.concatenate([np.arange(s0 - HAL, s0), np.arange(s0 + UT, s0 + UT + HAL)])
    cols = np.concatenate([own[0::2], own[1::2], halo[0::2], halo[1::2]]).astype(np.int64)
    ph = (rows[:, None] * cols[None, :]) % S
    ang = ph.astype(np.float64) * (2.0 * np.pi / S)
    sc = 1.0 / np.sqrt(S * 128.0)
    return np.stack([(np.cos(ang) * sc).astype(_BF), (np.sin(ang) * sc).astype(_BF)])


def _consts():
    if "c" in _CACHE:
        return _CACHE["c"]
    c = {}
    c["tabP"] = {}
    for hf in range(2):
        t = []
        for u in range(2):
            s0 = hf * 2048 + u * UT
            oth = (1 - hf) * 2048
            t.append(_tables(4096, s0, [np.arange(hf * 2048, hf * 2048 + 2048), np.arange(oth, oth + 2048)]))
        c["tabP"][hf] = np.ascontiguousarray(np.stack(t))
    t = []
    for pos in range(2):
        s0 = pos * UT
        t.append(_tables(2048, s0, [np.arange(0, 2048)]))
    c["tabS"] = np.ascontiguousarray(np.stack(t))
    j = np.arange(128, dtype=np.int64)
    ang = ((j[:, None] * j[None, :]) % 128).astype(np.float64) * (2.0 * np.pi / 128)
    c["c128"] = np.cos(ang).astype(_BF)
    c["s128n"] = (-np.sin(ang)).astype(_BF)
    c["ident"] = np.eye(128, dtype=np.float32)
    c["identb"] = np.eye(128, dtype=np.float32).astype(_BF)
    _CACHE["c"] = c
    return c


def _unit_meta(S, s0):
    corr = np.zeros((KC, 16), np.float32)
    for dc in range(KC):
        w = 2 << (dc // 2)
        for i in range(8):
            for (col, p) in ((i, s0 + i), (8 + i, s0 + UT - 8 + i)):
                lo = max(p - w // 2, 0)
                hi = min(p + w - w // 2, S)
                corr[dc, col] = 1.0 / float(hi - lo)
    vm = np.zeros((32,), np.float32)
    vm[0:16] = 1.0 if s0 > 0 else 0.0
    vm[16:32] = 1.0 if s0 + UT < S else 0.0
    return corr, vm


def _halo(xs, s0):
    S = xs.shape[0]
    out = np.zeros((64, D), np.float32)
    lo = s0 - 32
    if lo >= 0:
        out[0:32] = xs[lo:s0]
    hi = s0 + UT
    if hi + 32 <= S:
        out[32:64] = xs[hi:hi + 32]
    return out


def _fm(v, nch):
    return np.ascontiguousarray(np.asarray(v, np.float32).reshape(nch, 128).T)


def kernel(x_prompt, x_sample, mem_prompt, mem_sample, norm_mix, w_in_even, conv_w, conv_b, conv_ln_g, conv_ln_b,
           w_out_even, w_pool, pool_scale, norm_xa, norm_mem, xa_wq, xa_wkv, xa_wo, norm_ffn, ffn_w_gate_up,
           ffn_w_down, norm_final):
    f32 = lambda a: np.ascontiguousarray(np.asarray(a, dtype=np.float32))
    x_prompt, x_sample, mem_prompt, mem_sample = f32(x_prompt), f32(x_sample), f32(mem_prompt), f32(mem_sample)
    cst = _consts()
    vecs = np.zeros((128, NVEC), np.float32)
    for l in range(2):
        vecs[:, V_NMIX + l * 8:V_NMIX + l * 8 + 8] = _fm(norm_mix[l], 8)
        vecs[:, V_NXA + l * 8:V_NXA + l * 8 + 8] = _fm(norm_xa[l], 8)
        vecs[:, V_NMEM + l * 8:V_NMEM + l * 8 + 8] = _fm(norm_mem[l], 8)
        vecs[:, V_NFFN + l * 8:V_NFFN + l * 8 + 8] = _fm(norm_ffn[l], 8)
    vecs[:, V_NFIN:V_NFIN + 8] = _fm(norm_final, 8)
    vecs[:, V_PSC:V_PSC + 8] = _fm(np.asarray(pool_scale)[0], 8)
    vecs[:, V_CB:V_CB + 4] = _fm(np.asarray(conv_b)[0], 4)
    vecs[:, V_LG:V_LG + 4] = _fm(np.asarray(conv_ln_g)[0], 4)
    vecs[:, V_LB:V_LB + 4] = _fm(np.asarray(conv_ln_b)[0], 4)
    cw = np.asarray(conv_w, np.float32)[0]
    for c in range(4):
        vecs[:, V_CW + c * CW:V_CW + (c + 1) * CW] = cw[:, c * 128:(c + 1) * 128].T
    shared = {
        "vecs": vecs, "ident": cst["ident"], "identb": cst["identb"], "c128": cst["c128"], "s128n": cst["s128n"],
        "tabS": cst["tabS"],
        "w_in_even": f32(w_in_even), "w_out_even": f32(w_out_even), "w_pool": f32(w_pool), "xa_wq": f32(xa_wq),
        "xa_wkv": f32(xa_wkv), "xa_wo": f32(xa_wo), "ffn_w_gate_up": f32(ffn_w_gate_up), "ffn_w_down": f32(ffn_w_down),
    }
    in_maps = []
    for c in range(NCORES):
        pi, hf = c // 2, c % 2
        xu = np.empty((NUNIT, UT, D), np.float32)
        xh = np.empty((NUNIT, 64, D), np.float32)
        corr = np.empty((NUNIT, 128, 128), np.float32)
        vmask = np.empty((NUNIT, 128, 32), np.float32)
        for u in range(NUNIT):
            if u < 2:
                xs, S, s0 = x_prompt[pi], 4096, hf * 2048 + u * UT
            else:
                xs, S, s0 = x_sample[4 * c + (u - 2) // 2], 2048, ((u - 2) % 2) * UT
            xu[u] = xs[s0:s0 + UT]
            xh[u] = _halo(xs, s0)
            cr, vm = _unit_meta(S, s0)
            corr[u] = np.broadcast_to(cr.reshape(1, 128), (128, 128))
            vmask[u] = np.broadcast_to(vm.reshape(1, 32), (128, 32))
        m = dict(shared)
        m.update({
            "xu": xu, "xh": xh, "corr": corr, "vmask": vmask,
            "xop": np.ascontiguousarray(x_prompt[pi, (1 - hf) * 2048:(2 - hf) * 2048]),
            "mem": np.ascontiguousarray(np.concatenate([mem_prompt[pi:pi + 1], mem_sample[4 * c:4 * c + 4]], axis=0)),
            "tabP": cst["tabP"][hf],
        })
        in_maps.append(m)
    if "nc" not in _PROG:
        _PROG["nc"] = build_program()
    res = run_bass_kernel_spmd(_PROG["nc"], in_maps, core_ids=list(range(NCORES)))
    y_prompt = np.empty((4, 4096, D), np.float32)
    y_sample = np.empty((32, 2048, D), np.float32)
    for c in range(NCORES):
        yu = np.asarray(res.results[c]["yu"], dtype=np.float32)
        pi, hf = c // 2, c % 2
        for u in range(NUNIT):
            if u < 2:
                s0 = hf * 2048 + u * UT
                y_prompt[pi, s0:s0 + UT] = yu[u]
            else:
                s0 = ((u - 2) % 2) * UT
                y_sample[4 * c + (u - 2) // 2, s0:s0 + UT] = yu[u]
    return (y_prompt, y_sample)
```
